# Optimizing a Trainium2 kernel written in Bass

```python
import math
import jax, jax.numpy as jnp
from jax import lax
import numpy as np

D_MODEL = 1024
BATCH = 8
SEQ = 4096
DEPTH = 1

GRID_W = 64
CTX_LEN = 256
HEAD_DIM = 64
ROPE_THETA = 10000.0
NORM_EPS = 1e-6
Q_BLOCK = 128
DIFF_HEADS = 4
DIFF_V_DIM = 2 * HEAD_DIM
DIFF_WIDTH = DIFF_HEADS * DIFF_V_DIM
GQA_Q_HEADS = 8
GQA_KV_HEADS = 2
GQA_GROUP = GQA_Q_HEADS // GQA_KV_HEADS
GQA_WIDTH = GQA_Q_HEADS * HEAD_DIM
MIX_WIDTH = DIFF_WIDTH + GQA_WIDTH
GQA_KV_WIDTH = GQA_KV_HEADS * HEAD_DIM
IN_COLS = 3 * DIFF_WIDTH + GQA_WIDTH + 2 * GQA_KV_WIDTH
_IN_SPLITS = (DIFF_WIDTH, 2 * DIFF_WIDTH, 3 * DIFF_WIDTH,
              3 * DIFF_WIDTH + GQA_WIDTH, 3 * DIFF_WIDTH + GQA_WIDTH + GQA_KV_WIDTH)
PEER_HEADS = 8
PEER_N_KEYS = 128
PEER_N_EXPERTS = PEER_N_KEYS * PEER_N_KEYS
PEER_QUERY_DIM = 256
PEER_HALF = PEER_QUERY_DIM // 2
PEER_TOPK = 16
PEER_CHUNK = 128

kernel_name = "hymba_diffattn_gqa_peer_dit"


def rmsnorm(x, g):
    xf = x.astype(jnp.float32)
    y = xf * lax.rsqrt(jnp.mean(xf * xf, axis=-1, keepdims=True) + NORM_EPS)
    return (y * g.astype(jnp.float32)).astype(x.dtype)


def modulate(h, shift, scale):
    return h * (1 + scale) + shift


def _rope_axis(x, pos):
    d = x.shape[-1]
    half = d // 2
    freqs = ROPE_THETA ** (-jnp.arange(half, dtype=jnp.float32) / half)
    ang = pos.astype(jnp.float32)[:, None] * freqs[None, :]
    shape = (1, x.shape[1]) + (1,) * (x.ndim - 3) + (half,)
    cos = jnp.cos(ang).reshape(shape).astype(x.dtype)
    sin = jnp.sin(ang).reshape(shape).astype(x.dtype)
    x1, x2 = x[..., :half], x[..., half:]
    return jnp.concatenate([x1 * cos - x2 * sin, x1 * sin + x2 * cos], axis=-1)


def rope2d(x):
    n = x.shape[1]
    rows = n // GRID_W
    row_pos = jnp.broadcast_to(jnp.arange(rows, dtype=jnp.int32)[:, None], (rows, GRID_W)).reshape(-1)
    col_pos = jnp.broadcast_to(jnp.arange(GRID_W, dtype=jnp.int32)[None, :], (rows, GRID_W)).reshape(-1)
    d = x.shape[-1] // 2
    return jnp.concatenate([_rope_axis(x[..., :d], row_pos), _rope_axis(x[..., d:], col_pos)], axis=-1)


def attend(q, k, v):
    B, S, Hk, G, d = q.shape
    nb = S // Q_BLOCK
    qb = q.reshape(B, nb, Q_BLOCK, Hk, G, d).swapaxes(0, 1)
    scale = d ** -0.5

    def one(qblk):
        s = jnp.einsum('bqhgd,bkhd->bhgqk', qblk, k, preferred_element_type=jnp.float32) * scale
        p = jax.nn.softmax(s, axis=-1).astype(v.dtype)
        return jnp.einsum('bhgqk,bkhe->bqhge', p, v)

    o = lax.map(one, qb)
    return o.swapaxes(0, 1).reshape(B, S, Hk, G, v.shape[-1])


def _mixer_heads(p):
    B, S, _ = p.shape
    dq, dk, dv, gq, gk, gv = jnp.split(p, _IN_SPLITS, axis=-1)
    return (dq.reshape(B, S, DIFF_HEADS, 2, HEAD_DIM),
            dk.reshape(B, S, DIFF_HEADS, 2, HEAD_DIM),
            dv.reshape(B, S, DIFF_HEADS, DIFF_V_DIM),
            gq.reshape(B, S, GQA_Q_HEADS, HEAD_DIM),
            gk.reshape(B, S, GQA_KV_HEADS, HEAD_DIM),
            gv.reshape(B, S, GQA_KV_HEADS, HEAD_DIM))


def _mix_queries(dq, gq, dk, dv, gk, gv, lam, lambda_init, subln_g, w_out):
    B, S = dq.shape[:2]
    o1 = attend(dq[:, :, :, 0, None, :], dk[:, :, :, 0, :], dv)
    o2 = attend(dq[:, :, :, 1, None, :], dk[:, :, :, 1, :], dv)
    od = (o1 - lam.astype(o1.dtype) * o2)[:, :, :, 0, :]
    od = rmsnorm(od, subln_g) * (1.0 - lambda_init)
    og = attend(gq.reshape(B, S, GQA_KV_HEADS, GQA_GROUP, HEAD_DIM), gk, gv)
    o = jnp.concatenate([od.reshape(B, S, DIFF_WIDTH), og.reshape(B, S, GQA_WIDTH)], axis=-1)
    return o @ w_out


def peer(h, w_q, subkeys, u, v):
    B, S, D = h.shape
    xs = h.reshape(-1, PEER_CHUNK, D)

    def one(xc):
        C = xc.shape[0]
        q = (xc @ w_q).reshape(C, PEER_HEADS, 2, PEER_HALF)
        s = jnp.einsum('chpd,hpnd->chpn', q, subkeys, preferred_element_type=jnp.float32)
        sv, si = lax.top_k(s, PEER_TOPK)
        cand = sv[:, :, 0, :, None] + sv[:, :, 1, None, :]
        cand_id = si[:, :, 0, :, None] * PEER_N_KEYS + si[:, :, 1, None, :]
        best, pos = lax.top_k(cand.reshape(C, PEER_HEADS, PEER_TOPK * PEER_TOPK), PEER_TOPK)
        ids = jnp.take_along_axis(cand_id.reshape(C, PEER_HEADS, PEER_TOPK * PEER_TOPK), pos, axis=-1)
        g = jax.nn.softmax(best, axis=-1)
        ue = jnp.take(u, ids, axis=0)
        ve = jnp.take(v, ids, axis=0)
        a = jax.nn.gelu(jnp.einsum('cd,chkd->chk', xc, ue), approximate=False)
        return jnp.einsum('chk,chkd->cd', (g * a).astype(xc.dtype), ve)

    return lax.map(one, xs).reshape(B, S, D)


def setup_inputs(seed: int = 0) -> dict:
    key = jax.random.key(seed)
    ks = jax.random.split(key, 24)
    D = D_MODEL
    nrm = lambda k, shape, s: jax.random.normal(k, shape, jnp.float32) * s
    return {
        "x": nrm(ks[0], (BATCH, SEQ, D), 1.0),
        "c": nrm(ks[1], (BATCH, D), 1.0),
        "ctx": nrm(ks[2], (BATCH, CTX_LEN, D), 1.0),
        "c_ctx": nrm(ks[3], (D,), 1.0),
        "w_mod": nrm(ks[4], (DEPTH, D, 6 * D), 0.5 * D ** -0.5),
        "b_mod": nrm(ks[5], (DEPTH, 6 * D), 0.01),
        "norm1_g": 1.0 + nrm(ks[6], (DEPTH, D), 0.02),
        "norm2_g": 1.0 + nrm(ks[7], (DEPTH, D), 0.02),
        "w_in": nrm(ks[8], (DEPTH, D, IN_COLS), D ** -0.5),
        "w_out": nrm(ks[9], (DEPTH, MIX_WIDTH, D), MIX_WIDTH ** -0.5),
        "diff_lq1": nrm(ks[10], (DEPTH, HEAD_DIM), 0.1),
        "diff_lk1": nrm(ks[11], (DEPTH, HEAD_DIM), 0.1),
        "diff_lq2": nrm(ks[12], (DEPTH, HEAD_DIM), 0.1),
        "diff_lk2": nrm(ks[13], (DEPTH, HEAD_DIM), 0.1),
        "diff_subln_g": 1.0 + nrm(ks[14], (DEPTH, DIFF_V_DIM), 0.02),
        "gqa_q_norm_g": 1.0 + nrm(ks[15], (DEPTH, HEAD_DIM), 0.02),
        "gqa_k_norm_g": 1.0 + nrm(ks[16], (DEPTH, HEAD_DIM), 0.02),
        "peer_wq": nrm(ks[17], (DEPTH, D, PEER_HEADS * PEER_QUERY_DIM), D ** -0.5),
        "peer_subkeys": nrm(ks[18], (DEPTH, PEER_HEADS, 2, PEER_N_KEYS, PEER_HALF), PEER_HALF ** -0.5),
        "peer_u": nrm(ks[19], (DEPTH, PEER_N_EXPERTS, D), D ** -0.5),
        "peer_v": nrm(ks[20], (DEPTH, PEER_N_EXPERTS, D), (PEER_HEADS * PEER_TOPK) ** -0.5),
        "final_norm_g": 1.0 + nrm(ks[21], (D,), 0.02),
    }


def reference(x, c, ctx, c_ctx, w_mod, b_mod, norm1_g, norm2_g, w_in, w_out,
              diff_lq1, diff_lk1, diff_lq2, diff_lk2, diff_subln_g,
              gqa_q_norm_g, gqa_k_norm_g, peer_wq, peer_subkeys, peer_u, peer_v,
              final_norm_g):
    for i in range(DEPTH):
        last = i == DEPTH - 1
        lambda_init = 0.8 - 0.6 * math.exp(-0.3 * i)
        mod_x = (jax.nn.silu(c) @ w_mod[i] + b_mod[i])[:, None, :]
        mod_c = jax.nn.silu(c_ctx) @ w_mod[i] + b_mod[i]
        sh1x, sc1x, g1x, sh2x, sc2x, g2x = jnp.split(mod_x, 6, axis=-1)
        sh1c, sc1c, g1c, sh2c, sc2c, g2c = jnp.split(mod_c, 6, axis=-1)

        hx = modulate(rmsnorm(x, norm1_g[i]), sh1x, sc1x)
        hc = modulate(rmsnorm(ctx, norm1_g[i]), sh1c, sc1c)
        c_dq, c_dk, c_dv, c_gq, c_gk, c_gv = _mixer_heads(hc @ w_in[i])
        x_dq, x_dk, x_dv, x_gq, x_gk, x_gv = _mixer_heads(hx @ w_in[i])
        c_gq = rmsnorm(c_gq, gqa_q_norm_g[i])
        c_gk = rmsnorm(c_gk, gqa_k_norm_g[i])
        x_gq = rope2d(rmsnorm(x_gq, gqa_q_norm_g[i]))
        x_gk = rope2d(rmsnorm(x_gk, gqa_k_norm_g[i]))
        x_dq = rope2d(x_dq)
        x_dk = rope2d(x_dk)
        lam = (jnp.exp(jnp.sum(diff_lq1[i].astype(jnp.float32) * diff_lk1[i].astype(jnp.float32)))
               - jnp.exp(jnp.sum(diff_lq2[i].astype(jnp.float32) * diff_lk2[i].astype(jnp.float32)))
               + lambda_init)
        out_x = _mix_queries(x_dq, x_gq,
                             jnp.concatenate([c_dk, x_dk], axis=1),
                             jnp.concatenate([c_dv, x_dv], axis=1),
                             jnp.concatenate([c_gk, x_gk], axis=1),
                             jnp.concatenate([c_gv, x_gv], axis=1),
                             lam, lambda_init, diff_subln_g[i], w_out[i])
        if not last:
            out_c = _mix_queries(c_dq, c_gq, c_dk, c_dv, c_gk, c_gv,
                                 lam, lambda_init, diff_subln_g[i], w_out[i])
            ctx = ctx + g1c * out_c
        x = x + g1x * out_x

        hx2 = modulate(rmsnorm(x, norm2_g[i]), sh2x, sc2x)
        x = x + g2x * peer(hx2, peer_wq[i], peer_subkeys[i], peer_u[i], peer_v[i])
        if not last:
            hc2 = modulate(rmsnorm(ctx, norm2_g[i]), sh2c, sc2c)
            ctx = ctx + g2c * peer(hc2, peer_wq[i], peer_subkeys[i], peer_u[i], peer_v[i])
    return rmsnorm(x, final_norm_g)
```

```python
import os as _os
import numpy as np
from contextlib import ExitStack
import ml_dtypes
import concourse.bass as bass
import concourse.mybir as mybir
from concourse.bass_utils import run_bass_kernel_spmd

F32 = mybir.dt.float32
BF16 = mybir.dt.bfloat16
U32 = mybir.dt.uint32
AF = mybir.ActivationFunctionType
ALU = mybir.AluOpType
AX = mybir.AxisListType
ENGS = ["sync", "scalar", "vector", "gpsimd", "tensor"]
EPS = 1e-6
S = 4096
NT = 32
CTXL = 256
NKT = 34
LK = 4352
D = 1024


class Reg:
    __slots__ = ("name", "w", "r", "gdeps")

    def __init__(self, name):
        self.name = name
        self.w = []
        self.r = []
        self.gdeps = []


class Op:
    __slots__ = ("eng", "fn", "deps", "sig", "val", "key", "inc", "semkey", "seq")


class Prog:
    def __init__(self, nc, stack):
        self.nc = nc
        self.stack = stack
        self.q = {e: [] for e in ENGS}
        self.sems = {}
        self.last = {e: None for e in ENGS}
        self.dmas = []
        self.nseq = 0

    def op(self, eng, fn, reads=(), writes=(), wadd=(), after=(), semkey=None, soft=False, inc=1):
        o = Op()
        o.eng, o.fn, o.sig, o.val, o.key, o.inc, o.semkey = eng, fn, False, None, None, inc, semkey
        o.seq = self.nseq
        self.nseq += 1
        deps = []
        for r in reads:
            deps.extend(r.w)
        for r in writes:
            g = list(r.w) + list(r.r)
            r.gdeps = g
            deps.extend(g)
        for r in wadd:
            deps.extend(r.gdeps)
            deps.extend(r.r)
        deps.extend(after)
        seen = set()
        o.deps = []
        latest = {}
        for d in deps:
            if d is None or id(d) in seen:
                continue
            seen.add(id(d))
            if d.semkey is None:
                if d.eng == eng and (eng == "tensor" or soft):
                    continue
                if d.eng not in latest or latest[d.eng].seq < d.seq:
                    latest[d.eng] = d
                continue
            d.sig = True
            o.deps.append(d)
        for d in latest.values():
            d.sig = True
            o.deps.append(d)
        for r in reads:
            if semkey is None:
                r.r = [x for x in r.r if not (x.eng == eng and x.semkey is None)]
            r.r.append(o)
        for r in writes:
            r.w = [o]
            r.r = []
        for r in wadd:
            if semkey is None:
                r.w = [x for x in r.w if not (x.eng == eng and x.semkey is None)]
            r.w.append(o)
        self.q[eng].append(o)
        self.last[eng] = o
        if semkey is not None:
            self.dmas.append(o)
        return o

    def dma(self, eng, out, in_, semkey, reads=(), writes=(), wadd=(), after=(), **kw):
        return self.op(eng, lambda e: e.dma_start(out=out, in_=in_, **kw), reads, writes, wadd, after,
                       semkey=semkey, inc=16)

    def barrier(self):
        lasts = [self.last[e] for e in ENGS if self.last[e] is not None]
        pend = list(self.dmas)
        self.dmas = []
        st1 = [self.op(e, lambda en: en.nop(), after=lasts + pend) for e in ENGS]
        for e in ENGS:
            self.op(e, lambda en: en.nop(), after=st1)

    def emit(self):
        nc = self.nc
        cnt = {}
        for en in ENGS:
            for o in self.q[en]:
                if o.sig:
                    key = o.semkey if o.semkey is not None else ("E", en)
                    if key not in self.sems:
                        self.sems[key] = self.stack.enter_context(nc.semaphore("s%d" % len(self.sems)))
                    cnt[key] = cnt.get(key, 0) + o.inc
                    o.key, o.val = key, cnt[key]
        with nc.Block() as block:
            for en in ENGS:
                q = self.q[en]
                if not q:
                    continue

                def body(e, q=q):
                    waited = {}
                    for o in q:
                        need = {}
                        for d in o.deps:
                            if waited.get(d.key, 0) >= d.val:
                                continue
                            need[d.key] = max(need.get(d.key, 0), d.val)
                        for key, val in need.items():
                            waited[key] = val
                            e.wait_ge(self.sems[key], val)
                        ins = o.fn(e)
                        if o.sig:
                            ins.then_inc(self.sems[o.key], o.inc)

                getattr(block, en)(body)


def _rope_tables():
    half = 16
    freqs = (np.float32(10000.0) ** (-np.arange(half, dtype=np.float32) / np.float32(half))).astype(np.float32)
    t = np.arange(S)
    row = (t // 64).astype(np.float32)
    col = (t % 64).astype(np.float32)
    C = np.zeros((S, 64), np.float32)
    Sg = np.zeros((S, 64), np.float32)
    for a, pos in enumerate((row, col)):
        ang = (pos[:, None] * freqs[None, :]).astype(np.float32)
        c = np.cos(ang).astype(np.float32)
        s = np.sin(ang).astype(np.float32)
        C[:, a * 32:a * 32 + 16] = c
        C[:, a * 32 + 16:a * 32 + 32] = c
        Sg[:, a * 32:a * 32 + 16] = -s
        Sg[:, a * 32 + 16:a * 32 + 32] = s
    return C, Sg


def build(upto="PC2", dbg=()):
    nc = bass.Bass("TRN2", target_bir_lowering=False)
    phases = ["P0", "PA", "PB", "PC0", "PC1", "PC2"]
    phases = phases[:phases.index(upto) + 1]
    first_use = {"w_out": "PB", "peer_wq": "PC1", "subkeys": "PC1", "peer_u": "PC0", "peer_v": "PC0",
                 "iotab": "PC1", "iota16": "PC1", "w_in": "PA", "ropec": "PA", "ropes": "PA", "x": "PA", "ctx": "PA"}
    used = []

    def dt_in(name, shape, dt=F32):
        if first_use.get(name, "P0") not in phases:
            return None
        used.append(name)
        return nc.dram_tensor(name, shape, dt, kind="ExternalInput").ap()

    def scratch(name, shape, dt):
        kind = "ExternalOutput" if name in dbg else "Internal"
        return nc.dram_tensor(name, shape, dt, kind=kind).ap()

    x_d = dt_in("x", [S, D])
    ctx_d = dt_in("ctx", [CTXL, D])
    cc_d = dt_in("cc", [128, 8, 2])
    wmod_d = dt_in("w_mod", [D, 6 * D])
    bmod_d = dt_in("b_mod", [1, 6 * D])
    vecs_d = dt_in("vecs", [1, 8 * D])
    win_d = dt_in("w_in", [D, 2304])
    wout_d = dt_in("w_out", [D, D])
    wq_d = dt_in("peer_wq", [D, 2048])
    sk_d = dt_in("subkeys", [16, 128, 128])
    pu_d = dt_in("peer_u", [16384, D])
    pv_d = dt_in("peer_v", [16384, D])
    cos_d = dt_in("ropec", [S, 64])
    sin_d = dt_in("ropes", [S, 64])
    identb_d = dt_in("identb", [128, 128], BF16)
    identf_d = dt_in("identf", [128, 128])
    iotab_d = dt_in("iotab", [128, 128], BF16)
    iota16_d = dt_in("iota16", [128, 16])
    out_d = nc.dram_tensor("out", [S, D], F32, kind="ExternalOutput").ap() if "PC2" in phases else None

    modrows_d = scratch("modrows", [12, D], F32)
    QTd = scratch("QTd", [128, 16, S], BF16)
    X1d = scratch("X1d", [S, D], F32)
    H2Td = scratch("H2Td", [128, 8, S], BF16)
    IJGd = scratch("IJGd", [128, 3, S], F32)
    UTs = scratch("UTs", [128, 128, D], BF16)
    Vs = scratch("Vs", [128, 128, D], BF16)
    UVdbg = scratch("UVdbg", [4, 128, D], BF16) if "UVdbg" in dbg else None
    KTdbg = scratch("KTdbg", [128, 6, LK], BF16) if "KTdbg" in dbg else None
    Vdbg = scratch("Vdbg", [128, NKT, 646], BF16) if "Vdbg" in dbg else None

    nc._used_inputs = used

    with ExitStack() as st:
        p = Prog(nc, st)
        PSB = [st.enter_context(nc.psum_tensor("psb%d" % i, [128, 512], F32)) for i in range(8)]
        PR = [Reg("ps%d" % i) for i in range(8)]

        def sbuf(stack, name, shape, dt):
            return stack.enter_context(nc.sbuf_tensor("sb_" + name, shape, dt))

        identb = sbuf(st, "identb", [128, 128], BF16)
        identf = sbuf(st, "identf", [128, 128], F32)
        r_const = Reg("const")
        p.dma("sync", identb[:], identb_d, ("c", 0), writes=[r_const])
        p.dma("sync", identf[:], identf_d, ("c", 1), wadd=[r_const])

        def bcast_row(eng, dst, row, semkey, writes=(), wadd=()):
            return p.dma(eng, dst, row.partition_broadcast(128), semkey, writes=writes, wadd=wadd)

        if "P0" in phases:
            with ExitStack() as s0:
                cc = sbuf(s0, "cc", [128, 8, 2], F32)
                sc = sbuf(s0, "sc", [128, 8, 2], F32)
                wm = [sbuf(s0, "wm%d" % i, [128, 2048], F32) for i in range(2)]
                bmod = sbuf(s0, "bmod", [1, 6 * D], F32)
                vecs = sbuf(s0, "vecs", [1, 8 * D], F32)
                modx = sbuf(s0, "modx", [1, 6 * D], F32)
                modc = sbuf(s0, "modc", [1, 2 * D], F32)
                rows = sbuf(s0, "rows", [1, 12, D], F32)
                sm = sbuf(s0, "sm", [1, 16], F32)
                t64 = sbuf(s0, "t64", [1, 128], F32)
                r_cc, r_sc, r_bm, r_vec = Reg("cc"), Reg("sc"), Reg("bm"), Reg("vec")
                r_wm = [Reg("wm0"), Reg("wm1")]
                r_modx, r_modc, r_rows, r_sm, r_t64 = Reg("modx"), Reg("modc"), Reg("rows"), Reg("sm"), Reg("t64")
                p.dma("sync", cc[:], cc_d, ("p0", 0), writes=[r_cc])
                p.dma("sync", bmod[:], bmod_d, ("p0", 1), writes=[r_bm])
                p.dma("sync", vecs[:], vecs_d, ("p0", 2), writes=[r_vec])
                p.op("scalar", lambda e: e.activation(sc[:], cc[:], AF.Silu), reads=[r_cc], writes=[r_sc])
                it = 0
                for gi in range(3):
                    for kd in range(8):
                        sl = it % 2
                        it += 1
                        p.dma("sync", wm[sl][:], wmod_d[kd * 128:(kd + 1) * 128, gi * 2048:(gi + 1) * 2048],
                              ("wm", sl), writes=[r_wm[sl]])
                        for cb in range(4):
                            p.op("tensor", lambda e, sl=sl, kd=kd, cb=cb: e.matmul(
                                PSB[cb][0:1, :], sc[:, kd, 0:1], wm[sl][:, cb * 512:(cb + 1) * 512],
                                start=(kd == 0), stop=(kd == 7)),
                                reads=[r_sc, r_wm[sl]], writes=[PR[cb]] if kd == 0 else (), wadd=[PR[cb]] if kd else ())
                            if gi == 0:
                                p.op("tensor", lambda e, sl=sl, kd=kd, cb=cb: e.matmul(
                                    PSB[4 + cb][0:1, :], sc[:, kd, 1:2], wm[sl][:, cb * 512:(cb + 1) * 512],
                                    start=(kd == 0), stop=(kd == 7)),
                                    reads=[r_sc, r_wm[sl]], writes=[PR[4 + cb]] if kd == 0 else (),
                                    wadd=[PR[4 + cb]] if kd else ())
                    for cb in range(4):
                        c0 = gi * 2048 + cb * 512
                        p.op("vector", lambda e, cb=cb, c0=c0: e.tensor_tensor(
                            modx[0:1, c0:c0 + 512], PSB[cb][0:1, :], bmod[0:1, c0:c0 + 512], ALU.add),
                            reads=[PR[cb], r_bm], wadd=[r_modx])
                        if gi == 0:
                            p.op("vector", lambda e, cb=cb, c0=c0: e.tensor_tensor(
                                modc[0:1, c0:c0 + 512], PSB[4 + cb][0:1, :], bmod[0:1, c0:c0 + 512], ALU.add),
                                reads=[PR[4 + cb], r_bm], wadd=[r_modc])
                V = lambda k: vecs[0:1, k * D:(k + 1) * D]
                MX = lambda k: modx[0:1, k * D:(k + 1) * D]
                MC = lambda k: modc[0:1, k * D:(k + 1) * D]
                RW = lambda k: rows[0:1, k, :]

                def stt(out, in0, in1):
                    p.op("vector", lambda e: e.scalar_tensor_tensor(out, in0, 1.0, in1, ALU.add, ALU.mult),
                         reads=[r_modx, r_modc, r_vec], wadd=[r_rows])

                def cp(out, in_):
                    p.op("vector", lambda e: e.tensor_copy(out, in_), reads=[r_modx, r_modc, r_vec, r_sm],
                         wadd=[r_rows])

                stt(RW(0), MX(1), V(0))
                cp(RW(1), MX(0))
                stt(RW(2), MC(1), V(0))
                cp(RW(3), MC(0))
                cp(RW(4), MX(2))
                stt(RW(5), MX(4), V(1))
                cp(RW(6), MX(3))
                cp(RW(7), MX(5))
                cp(RW(8), V(2))
                L = vecs[0:1, 6 * D:6 * D + 256]
                p.op("vector", lambda e: e.tensor_tensor(t64[0:1, 0:64], L[:, 0:64], L[:, 64:128], ALU.mult),
                     reads=[r_vec], writes=[r_t64])
                p.op("vector", lambda e: e.tensor_tensor(t64[0:1, 64:128], L[:, 128:192], L[:, 192:256], ALU.mult),
                     reads=[r_vec], wadd=[r_t64])
                p.op("vector", lambda e: e.tensor_reduce(sm[0:1, 0:2], t64[0:1, :].rearrange("p (a b) -> p a b", a=2),
                                                         AX.X, ALU.add), reads=[r_t64], writes=[r_sm])
                p.op("scalar", lambda e: e.activation(sm[0:1, 2:4], sm[0:1, 0:2], AF.Exp), reads=[r_sm], wadd=[r_sm])
                p.op("vector", lambda e: e.tensor_tensor(sm[0:1, 4:5], sm[0:1, 2:3], sm[0:1, 3:4], ALU.subtract),
                     reads=[r_sm], wadd=[r_sm])
                p.op("vector", lambda e: e.memset(rows[0:1, 9, :], 0.0), wadd=[r_rows])
                p.op("vector", lambda e: e.tensor_scalar(rows[0:1, 9, 0:128], t64[0:1, :], 0.0, None, ALU.mult),
                     reads=[r_t64], wadd=[r_rows])
                p.op("vector", lambda e: e.tensor_scalar(sm[0:1, 5:6], sm[0:1, 4:5], -1.0, -0.2, ALU.mult, ALU.add),
                     reads=[r_sm], wadd=[r_sm])
                p.op("vector", lambda e: e.tensor_scalar(rows[0:1, 9, 0:128], rows[0:1, 9, 0:128], sm[0:1, 5:6], None,
                                                         ALU.add), reads=[r_sm, r_rows], wadd=[r_rows])
                p.op("vector", lambda e: e.tensor_scalar(rows[0:1, 9, 128:256], vecs[0:1, 3 * D:3 * D + 128], 0.8, None,
                                                         ALU.mult), reads=[r_vec], wadd=[r_rows])
                cp(rows[0:1, 9, 256:320], vecs[0:1, 4 * D:4 * D + 64])
                cp(rows[0:1, 9, 320:384], vecs[0:1, 5 * D:5 * D + 64])
                p.op("vector", lambda e: e.memset(rows[0:1, 10:12, :], 0.0), wadd=[r_rows])
                r_mrd = Reg("modrows_d")
                p.dma("sync", modrows_d.rearrange("(o r) d -> o r d", o=1), rows[:], ("p0", 3), reads=[r_rows],
                      writes=[r_mrd])
                p.barrier()
        if upto == "P0":
            p.emit()
            return nc

        sAB = st.enter_context(ExitStack())
        KT = sbuf(sAB, "KT", [128, 5, LK], BF16)
        Vsb = sbuf(sAB, "Vsb", [128, NKT, 646], BF16)
        r_KT = [Reg("KT%d" % n) for n in range(NKT)]
        r_V = [Reg("V%d" % n) for n in range(NKT)]

        with ExitStack() as sa:
            win = sbuf(sa, "win", [128, 8, 2304], BF16)
            xt = [sbuf(sa, "xt%d" % i, [128, D], F32) for i in range(2)]
            hb = sbuf(sa, "hb", [128, D], BF16)
            hT = [sbuf(sa, "hT%d" % i, [128, 8, 128], BF16) for i in range(2)]
            Rf = sbuf(sa, "Rf", [128, 1664], F32)
            T1 = sbuf(sa, "T1", [128, 1664], F32)
            T2 = sbuf(sa, "T2", [128, 832], F32)
            QKb = sbuf(sa, "QKb", [128, 1792], BF16)
            QTst = [sbuf(sa, "QTst%d" % i, [128, 16, 128], BF16) for i in range(2)]
            Amod = sbuf(sa, "Amod", [128, D], F32)
            Bmod = sbuf(sa, "Bmod", [128, D], F32)
            sq = sbuf(sa, "sq", [128, D], F32)
            Ct = [sbuf(sa, "Ct%d" % i, [128, 64], F32) for i in range(2)]
            St = [sbuf(sa, "St%d" % i, [128, 64], F32) for i in range(2)]
            QG = sbuf(sa, "QG", [128, 64], F32)
            KG = sbuf(sa, "KG", [128, 64], F32)
            small = sbuf(sa, "smallA", [128, 40], F32)
            r_win, r_hb, r_Rf, r_T1, r_T2, r_QKb = Reg("win"), Reg("hb"), Reg("Rf"), Reg("T1"), Reg("T2"), Reg("QKb")
            r_xt = [Reg("xt0"), Reg("xt1")]
            r_hT = [Reg("hT0"), Reg("hT1")]
            r_QTst = [Reg("QTst0"), Reg("QTst1")]
            r_AB, r_sq, r_QKG = Reg("AB"), Reg("sq"), Reg("QKG")
            r_rope = [Reg("rope0"), Reg("rope1")]
            r_ss, r_sd, r_rs, r_ssh, r_sdh, r_rsh = (Reg(n) for n in ("ss", "sd", "rs", "ssh", "sdh", "rsh"))
            r_QTd = Reg("QTd")
            for kd in range(8):
                for hh in range(2):
                    p.dma("gpsimd", win[:, kd, hh * 1152:(hh + 1) * 1152],
                          win_d[kd * 128:(kd + 1) * 128, hh * 1152:(hh + 1) * 1152], ("win", 0),
                          writes=[r_win] if (kd == 0 and hh == 0) else (), wadd=() if (kd == 0 and hh == 0) else [r_win])
            for n in range(NKT):
                pass
            p.op("gpsimd", lambda e: e.memset(Vsb[:, :, 0:516].rearrange("p n (h c) -> p n h c", c=129)[:, :, :, 128:129], 1.0),
                 writes=r_V)
            p.op("gpsimd", lambda e: e.memset(Vsb[:, :, 516:646].rearrange("p n (h c) -> p n h c", c=65)[:, :, :, 64:65], 1.0),
                 wadd=r_V)
            for i_ in range(2):
                p.op("gpsimd", lambda e, i_=i_: e.memset(QTst[i_][:], 0.0), writes=[r_QTst[i_]])
            bcast_row("sync", QG[:], modrows_d[9:10, 256:320], ("pa", 0), writes=[r_QKG])
            bcast_row("sync", KG[:], modrows_d[9:10, 320:384], ("pa", 1), wadd=[r_QKG])

            for n in range(int(_os.environ.get('NTILES', NKT))):
                is_ctx = n < 2
                sl = n % 2
                t0 = (n - 2) * 128
                src = ctx_d[n * 128:(n + 1) * 128, :] if is_ctx else x_d[t0:t0 + 128, :]
                if n == 0 or n == 2:
                    ra, rb = (2, 3) if is_ctx else (0, 1)
                    bcast_row("sync", Amod[:], modrows_d[ra:ra + 1, :], ("pa", 2), writes=[r_AB])
                    bcast_row("sync", Bmod[:], modrows_d[rb:rb + 1, :], ("pa", 3), wadd=[r_AB])
                p.dma("sync", xt[sl][:], src, ("xt", sl), writes=[r_xt[sl]])
                if not is_ctx:
                    p.dma("gpsimd", Ct[sl][:], cos_d[t0:t0 + 128, :], ("rope", sl), writes=[r_rope[sl]])
                    p.dma("gpsimd", St[sl][:], sin_d[t0:t0 + 128, :], ("rope", sl), wadd=[r_rope[sl]])
                p.op("scalar", lambda e, sl=sl: e.activation(sq[:], xt[sl][:], AF.Square, accum_out=small[:, 0:1]),
                     reads=[r_xt[sl]], writes=[r_sq, r_ss])
                p.op("scalar", lambda e: e.activation(small[:, 1:2], small[:, 0:1], AF.Sqrt, bias=EPS, scale=1.0 / D),
                     reads=[r_ss], writes=[r_sd])
                p.op("vector", lambda e: e.reciprocal(small[:, 2:3], small[:, 1:2]), reads=[r_sd], writes=[r_rs])
                p.op("vector", lambda e, sl=sl: e.scalar_tensor_tensor(T1[:, 0:D], xt[sl][:], small[:, 2:3], Amod[:],
                                                                      ALU.mult, ALU.mult),
                     reads=[r_xt[sl], r_rs, r_AB], writes=[r_T1])
                p.op("vector", lambda e: e.tensor_tensor(hb[:], T1[:, 0:D], Bmod[:], ALU.add),
                     reads=[r_T1, r_AB], writes=[r_hb])
                pb0 = PSB[0][:].bitcast(BF16).rearrange("p (k t) -> p k t", k=8)
                for kd in range(8):
                    p.op("tensor", lambda e, kd=kd: e.transpose(pb0[:, kd, :], hb[:, kd * 128:(kd + 1) * 128], identb[:]),
                         reads=[r_hb, r_const], writes=[PR[0]] if kd == 0 else (), wadd=[PR[0]] if kd else ())
                p.op("scalar", lambda e, sl=sl: e.copy(hT[sl][:], pb0), reads=[PR[0]], writes=[r_hT[sl]])
                blocks = [(1, 512, 512), (2, 1024, 512), (4, 2048, 256)] if is_ctx else \
                    [(0, 0, 512), (1, 512, 512), (2, 1024, 512), (3, 1536, 512), (4, 2048, 256)]
                for (b, c0, w) in blocks:
                    for kd in range(8):
                        p.op("tensor", lambda e, b=b, c0=c0, w=w, kd=kd, sl=sl: e.matmul(
                            PSB[1 + b][:, 0:w], hT[sl][:, kd, :], win[:, kd, c0:c0 + w], start=(kd == 0), stop=(kd == 7)),
                            reads=[r_hT[sl], r_win], writes=[PR[1 + b]] if kd == 0 else (), wadd=[PR[1 + b]] if kd else ())
                Pdq, Pdk, Pdv, Pgq, Pg = PSB[1], PSB[2], PSB[3], PSB[4], PSB[5]
                p.op("scalar", lambda e, n=n: e.copy(
                    Vsb[:, n, 0:516].rearrange("p (h c) -> p h c", c=129)[:, :, 0:128],
                    Pdv[:].rearrange("p (h c) -> p h c", c=128)), reads=[PR[3]], wadd=[r_V[n]])
                p.op("scalar", lambda e, n=n: e.copy(
                    Vsb[:, n, 516:646].rearrange("p (h c) -> p h c", c=65)[:, :, 0:64],
                    Pg[:, 128:256].rearrange("p (h c) -> p h c", c=64)), reads=[PR[5]], wadd=[r_V[n]])
                nh = 2 if is_ctx else 10
                if not is_ctx:
                    p.op("scalar", lambda e: e.activation(sq[:, 0:512], Pgq[:], AF.Square), reads=[PR[4]], writes=[r_sq])
                    p.op("scalar", lambda e: e.activation(sq[:, 512:640], Pg[:, 0:128], AF.Square), reads=[PR[5]],
                         wadd=[r_sq])
                    sqv = sq[:, 0:640]
                else:
                    p.op("scalar", lambda e: e.activation(sq[:, 0:128], Pg[:, 0:128], AF.Square), reads=[PR[5]],
                         writes=[r_sq])
                    sqv = sq[:, 0:128]
                p.op("vector", lambda e, sqv=sqv, nh=nh: e.tensor_reduce(
                    small[:, 4:4 + nh], sqv.rearrange("p (h c) -> p h c", c=64), AX.X, ALU.add),
                    reads=[r_sq], writes=[r_ssh])
                p.op("scalar", lambda e, nh=nh: e.activation(small[:, 14:14 + nh], small[:, 4:4 + nh], AF.Sqrt, bias=EPS,
                                                             scale=1.0 / 64), reads=[r_ssh], writes=[r_sdh])
                p.op("vector", lambda e, nh=nh: e.reciprocal(small[:, 24:24 + nh], small[:, 14:14 + nh]),
                     reads=[r_sdh], writes=[r_rsh])
                if is_ctx:
                    p.op("scalar", lambda e: e.copy(QKb[:, 512:1024], Pdk[:]), reads=[PR[2]], writes=[r_QKb])
                    p.op("vector", lambda e: e.tensor_tensor(
                        Rf[:, 1536:1664].rearrange("p (h c) -> p h c", c=64),
                        Pg[:, 0:128].rearrange("p (h c) -> p h c", c=64),
                        small[:, 24:26].unsqueeze(2).broadcast_to([128, 2, 64]), ALU.mult),
                        reads=[PR[5], r_rsh], writes=[r_Rf])
                    p.op("vector", lambda e: e.tensor_tensor(
                        QKb[:, 1536:1664].rearrange("p (h c) -> p h c", c=64),
                        Rf[:, 1536:1664].rearrange("p (h c) -> p h c", c=64),
                        KG[:].unsqueeze(1).broadcast_to([128, 2, 64]), ALU.mult),
                        reads=[r_Rf, r_QKG], wadd=[r_QKb])
                else:
                    p.op("scalar", lambda e: e.copy(Rf[:, 0:512], Pdq[:]), reads=[PR[1]], writes=[r_Rf])
                    p.op("scalar", lambda e: e.copy(Rf[:, 512:1024], Pdk[:]), reads=[PR[2]], wadd=[r_Rf])
                    p.op("vector", lambda e: e.tensor_tensor(
                        T1[:, 1024:1536].rearrange("p (h c) -> p h c", c=64),
                        Pgq[:].rearrange("p (h c) -> p h c", c=64),
                        small[:, 24:32].unsqueeze(2).broadcast_to([128, 8, 64]), ALU.mult),
                        reads=[PR[4], r_rsh], writes=[r_T1])
                    p.op("vector", lambda e: e.tensor_tensor(
                        T1[:, 1536:1664].rearrange("p (h c) -> p h c", c=64),
                        Pg[:, 0:128].rearrange("p (h c) -> p h c", c=64),
                        small[:, 24 + 8:24 + 10].unsqueeze(2).broadcast_to([128, 2, 64]), ALU.mult),
                        reads=[PR[5], r_rsh], wadd=[r_T1])
                    p.op("vector", lambda e: e.tensor_tensor(
                        Rf[:, 1024:1536].rearrange("p (h c) -> p h c", c=64),
                        T1[:, 1024:1536].rearrange("p (h c) -> p h c", c=64),
                        QG[:].unsqueeze(1).broadcast_to([128, 8, 64]), ALU.mult),
                        reads=[r_T1, r_QKG], wadd=[r_Rf])
                    p.op("vector", lambda e: e.tensor_tensor(
                        Rf[:, 1536:1664].rearrange("p (h c) -> p h c", c=64),
                        T1[:, 1536:1664].rearrange("p (h c) -> p h c", c=64),
                        KG[:].unsqueeze(1).broadcast_to([128, 2, 64]), ALU.mult),
                        reads=[r_T1, r_QKG], wadd=[r_Rf])
                    R3 = Rf[:].rearrange("p (h c) -> p h c", c=64)
                    p.op("vector", lambda e, sl=sl: e.tensor_tensor(
                        T1[:].rearrange("p (h c) -> p h c", c=64), R3,
                        Ct[sl][:].unsqueeze(1).broadcast_to([128, 26, 64]), ALU.mult),
                        reads=[r_Rf, r_rope[sl]], writes=[r_T1])
                    R4 = Rf[:].rearrange("p (h a s f) -> p h a s f", a=2, s=2, f=16)
                    T4 = T1[:].rearrange("p (h a s f) -> p h a s f", a=2, s=2, f=16)
                    Q4 = QKb[:, 0:1664].rearrange("p (h a s f) -> p h a s f", a=2, s=2, f=16)
                    T2v = T2[:].rearrange("p (h a f) -> p h a f", a=2, f=16)
                    for half in range(2):
                        S4 = St[sl][:].rearrange("p (a s f) -> p a s f", a=2, s=2, f=16)[:, :, half, :]
                        p.op("vector", lambda e, half=half, S4=S4: e.tensor_tensor(
                            T2v, R4[:, :, :, 1 - half, :], S4.unsqueeze(1).broadcast_to([128, 26, 2, 16]), ALU.mult),
                            reads=[r_Rf, r_rope[sl]], writes=[r_T2])
                        p.op("vector", lambda e, half=half: e.tensor_tensor(
                            Q4[:, 0:16, :, half, :], T4[:, 0:16, :, half, :], T2v[:, 0:16], ALU.add),
                            reads=[r_T1, r_T2], writes=[r_QKb] if half == 0 else (), wadd=[r_QKb] if half else ())
                        p.op("vector", lambda e, half=half: e.tensor_tensor(
                            Q4[:, 24:26, :, half, :], T4[:, 24:26, :, half, :], T2v[:, 24:26], ALU.add),
                            reads=[r_T1, r_T2], wadd=[r_QKb])
                        G6 = QKb[:, 1024:1536].rearrange("p (j k a s f) -> p j k a s f", j=4, k=2, a=2, s=2, f=16)
                        for kvh in range(2):
                            h0 = 16 + kvh * 4
                            p.op("vector", lambda e, half=half, kvh=kvh, h0=h0, G6=G6: e.tensor_tensor(
                                G6[:, :, kvh, :, half, :], T4[:, h0:h0 + 4, :, half, :], T2v[:, h0:h0 + 4], ALU.add),
                                reads=[r_T1, r_T2], wadd=[r_QKb])
                pk = PSB[7][:].bitcast(BF16).rearrange("p (k t) -> p k t", k=8)
                pq = PSB[6][:].bitcast(BF16).rearrange("p (k t) -> p k t", k=8)
                kch = [4, 5, 6, 7, 12]
                for j, ch in enumerate(kch):
                    p.op("tensor", lambda e, j=j, ch=ch: e.transpose(pk[:, j, :], QKb[:, ch * 128:(ch + 1) * 128], identb[:]),
                         reads=[r_QKb, r_const], writes=[PR[7]] if j == 0 else (), wadd=[PR[7]] if j else ())
                p.op("scalar", lambda e, n=n: e.copy(KT[:, :, n * 128:(n + 1) * 128], pk[:, 0:5, :]), reads=[PR[7]],
                     writes=[r_KT[n]])
                if not is_ctx:
                    qch = [0, 1, 2, 3, 8, 9, 10, 11]
                    for j, ch in enumerate(qch):
                        p.op("tensor", lambda e, j=j, ch=ch: e.transpose(pq[:, j, :], QKb[:, ch * 128:(ch + 1) * 128], identb[:]),
                             reads=[r_QKb, r_const], writes=[PR[6]] if j == 0 else (), wadd=[PR[6]] if j else ())
                    for cc_ in range(2):
                        lo = cc_ * 64
                        Qs = QTst[sl][lo:lo + 64, 0:8, :].rearrange("p (h c) t -> p h c t", c=2)[:, :, cc_, :]
                        p.op("vector", lambda e, lo=lo, Qs=Qs: e.tensor_copy(Qs, pq[lo:lo + 64, 0:4, :]), reads=[PR[6]],
                             writes=[r_QTst[sl]] if cc_ == 0 else (), wadd=[r_QTst[sl]] if cc_ else ())
                        p.op("vector", lambda e, lo=lo, cc_=cc_, sl=sl: e.tensor_copy(
                            QTst[sl][lo:lo + 64, 8 + cc_ * 4:12 + cc_ * 4, :], pq[lo:lo + 64, 4:8, :]), reads=[PR[6]],
                            wadd=[r_QTst[sl]])
                    p.dma("sync", QTd[:, :, t0:t0 + 128], QTst[sl][:], ("qtst", sl), reads=[r_QTst[sl]], wadd=[r_QTd])
            if KTdbg is not None:
                p.dma("sync", KTdbg, KT[:], ("dbg", 0), reads=r_KT)
            if Vdbg is not None:
                p.dma("sync", Vdbg, Vsb[:], ("dbg", 1), reads=r_V)
            p.barrier()
        if upto == "PA":
            p.emit()
            return nc
        with ExitStack() as sb_:
            wout = sbuf(sb_, "wout", [128, 8, D], BF16)
            QTb = [sbuf(sb_, "QTb%d" % i, [128, 8, 512], BF16) for i in range(2)]
            PT = [sbuf(sb_, "PT%d" % i, [128, 512], BF16) for i in range(4)]
            Ot = sbuf(sb_, "Ot", [128, 4, D], BF16)
            OT = sbuf(sb_, "OT", [128, 8, 128], BF16)
            o1n = sbuf(sb_, "o1n", [128, 4, 128], F32)
            od = sbuf(sb_, "od", [128, 4, 128], F32)
            junk = sbuf(sb_, "junk", [128, 4, 128], F32)
            xt2 = [sbuf(sb_, "xt2_%d" % i, [128, D], F32) for i in range(2)]
            tmpf = sbuf(sb_, "tmpf", [128, D], F32)
            x1t = [sbuf(sb_, "x1t%d" % i, [128, D], F32) for i in range(2)]
            h2b = sbuf(sb_, "h2b", [128, D], BF16)
            h2Tst = [sbuf(sb_, "h2Tst%d" % i, [128, 8, 128], BF16) for i in range(2)]
            G1 = sbuf(sb_, "G1", [128, D], F32)
            A2 = sbuf(sb_, "A2", [128, D], F32)
            B2 = sbuf(sb_, "B2", [128, D], F32)
            SG = sbuf(sb_, "SG", [128, 128], F32)
            lamneg = sbuf(sb_, "lamneg", [128, 1], F32)
            smb = sbuf(sb_, "smb", [128, 64], F32)
            r_wout, r_Ot, r_OT, r_o1n, r_od, r_junk = (Reg(n) for n in ("wout", "Ot", "OT", "o1n", "od", "junk"))
            r_QTb = [Reg("QTb0"), Reg("QTb1")]
            r_PT = [Reg("PT%d" % i) for i in range(4)]
            r_xt2 = [Reg("xt2_0"), Reg("xt2_1")]
            r_x1t = [Reg("x1t0"), Reg("x1t1")]
            r_h2Tst = [Reg("h2Tst0"), Reg("h2Tst1")]
            r_tmpf, r_h2b, r_cB, r_smb, r_smb2 = (Reg(n) for n in ("tmpf", "h2b", "cB", "smb", "smb2"))
            r_X1d, r_H2Td = Reg("X1d"), Reg("H2Td")
            for kc in range(8):
                p.dma("gpsimd", wout[:, kc, :], wout_d[kc * 128:(kc + 1) * 128, :], ("wout", 0),
                      writes=[r_wout] if kc == 0 else (), wadd=[r_wout] if kc else ())
            bcast_row("sync", G1[:], modrows_d[4:5, :], ("pb", 0), writes=[r_cB])
            bcast_row("sync", A2[:], modrows_d[5:6, :], ("pb", 1), wadd=[r_cB])
            bcast_row("sync", B2[:], modrows_d[6:7, :], ("pb", 2), wadd=[r_cB])
            bcast_row("sync", SG[:], modrows_d[9:10, 128:256], ("pb", 3), wadd=[r_cB])
            bcast_row("sync", lamneg[:], modrows_d[9:10, 0:1], ("pb", 4), wadd=[r_cB])

            heads = []
            for h in range(4):
                for c in range(2):
                    heads.append(dict(kind="d", h=h, c=c, kch=h, qch=h, base=c * 64, v0=h * 129, vw=129))
            for g in range(8):
                kv = g // 4
                b = (g % 2) * 64
                heads.append(dict(kind="g", g=g, kch=4, qch=4 + g // 2, base=b,
                                  v0=516 + kv * 65, vw=65))
            pairs = [(3, 4), (5, 6)]
            NQB = int(_os.environ.get("NQB", 8))
            for qb in range(NQB):
                slq = 0

                def load_q(part, qbn):
                    p.dma("sync", QTb[part][:], QTd[:, part * 8:(part + 1) * 8, qbn * 512:(qbn + 1) * 512], ("qtb", part),
                          reads=[r_QTd], writes=[r_QTb[part]])

                if qb == 0:
                    load_q(0, 0)
                    load_q(1, 0)
                steps = [(hi, c) for hi in range(len(heads)) for c in range(NKT)]

                def emit_qk(k, slq=slq):
                    hi, c = steps[k]
                    hd = heads[hi]
                    kch = hd["kch"]
                    part, m = (0, hi) if hi < 8 else (1, hi - 8)
                    sbk, pt = k % 3, k % 4
                    p.op("tensor", lambda e: e.matmul(
                        PSB[sbk][:], KT[:, kch, c * 128:(c + 1) * 128], QTb[part][:, m, :], start=True, stop=True),
                        reads=[r_KT[c], r_QTb[part]], writes=[PR[sbk]])
                    if qb + 1 < NQB and c == NKT - 1 and hi in (7, 15):
                        load_q(part, qb + 1)
                    p.op("scalar", lambda e: e.activation(PT[pt][:], PSB[sbk][:], AF.Exp, scale=0.125),
                         reads=[PR[sbk]], writes=[r_PT[pt]])

                def emit_pv(k):
                    hi, c = steps[k]
                    hd = heads[hi]
                    pair = pairs[hi % 2]
                    v0, vw = hd["v0"], hd["vw"]
                    pt = k % 4
                    for qs in range(4):
                        bank = pair[qs // 2]
                        c0 = (qs % 2) * vw
                        first = (c == 0 and qs % 2 == 0)
                        p.op("tensor", lambda e, bank=bank, c0=c0, qs=qs, first=first: e.matmul(
                            PSB[bank][:, c0:c0 + vw], PT[pt][:, qs * 128:(qs + 1) * 128], Vsb[:, c, v0:v0 + vw],
                            start=first, stop=(c == NKT - 1), skip_group_check=True),
                            reads=[r_PT[pt], r_V[c]], writes=[PR[bank]] if first else (), wadd=() if first else [PR[bank]])

                LA = 2
                for k in range(LA):
                    emit_qk(k)
                for k in range(len(steps)):
                    if k + LA < len(steps):
                        emit_qk(k + LA)
                    emit_pv(k)
                    hi, c = steps[k]
                    if c != NKT - 1:
                        continue
                    hd = heads[hi]
                    pair = pairs[hi % 2]
                    vw = hd["vw"]
                    vA = PSB[pair[0]][:, 0:2 * vw].rearrange("p (q c) -> p q c", c=vw)
                    vB = PSB[pair[1]][:, 0:2 * vw].rearrange("p (q c) -> p q c", c=vw)
                    vv = [vA, vA, vB, vB]
                    dv_ = vw - 1
                    rpair = [PR[pair[0]], PR[pair[1]]]
                    if hd["kind"] == "d" and hd["c"] == 0:
                        p.op("vector", lambda e, vA=vA, dv_=dv_: e.reciprocal(smb[:, 0:2], vA[:, :, dv_]), reads=rpair, writes=[r_smb])
                        p.op("vector", lambda e, vB=vB, dv_=dv_: e.reciprocal(smb[:, 2:4], vB[:, :, dv_]), reads=rpair, wadd=[r_smb])
                        for qs in range(4):
                            p.op("vector", lambda e, qs=qs, vv=vv: e.tensor_scalar(
                                o1n[:, qs, :], vv[qs][:, qs % 2, 0:128], smb[:, qs:qs + 1], None, ALU.mult),
                                reads=rpair + [r_smb], writes=[r_o1n] if qs == 0 else (), wadd=[r_o1n] if qs else ())
                    elif hd["kind"] == "d":
                        h = hd["h"]
                        p.op("vector", lambda e, vA=vA, dv_=dv_: e.reciprocal(smb[:, 4:6], vA[:, :, dv_]), reads=rpair, writes=[r_smb])
                        p.op("vector", lambda e, vB=vB, dv_=dv_: e.reciprocal(smb[:, 6:8], vB[:, :, dv_]), reads=rpair, wadd=[r_smb])
                        p.op("vector", lambda e: e.tensor_scalar(smb[:, 8:12], smb[:, 4:8], lamneg[:, 0:1], None, ALU.mult),
                             reads=[r_smb, r_cB], wadd=[r_smb])
                        for qs in range(4):
                            p.op("vector", lambda e, qs=qs, vv=vv: e.scalar_tensor_tensor(
                                od[:, qs, :], vv[qs][:, qs % 2, 0:128], smb[:, 8 + qs:9 + qs], o1n[:, qs, :], ALU.mult, ALU.add),
                                reads=rpair + [r_smb, r_o1n], writes=[r_od] if qs == 0 else (), wadd=[r_od] if qs else ())
                        p.op("vector", lambda e: e.tensor_tensor(junk[:], od[:], od[:], ALU.mult), reads=[r_od], writes=[r_junk])
                        p.op("vector", lambda e: e.tensor_reduce(smb[:, 12:16], junk[:], AX.X, ALU.add), reads=[r_junk], wadd=[r_smb])
                        p.op("scalar", lambda e: e.activation(smb[:, 16:20], smb[:, 12:16], AF.Sqrt, bias=EPS, scale=1.0 / 128),
                             reads=[r_smb], wadd=[r_smb])
                        p.op("vector", lambda e: e.reciprocal(smb[:, 20:24], smb[:, 16:20]), reads=[r_smb], wadd=[r_smb])
                        for qs in range(4):
                            p.op("vector", lambda e, qs=qs, h=h: e.scalar_tensor_tensor(
                                Ot[:, qs, h * 128:(h + 1) * 128], od[:, qs, :], smb[:, 20 + qs:21 + qs], SG[:], ALU.mult, ALU.mult),
                                reads=[r_od, r_smb, r_cB], wadd=[r_Ot])
                    else:
                        g = hd["g"]
                        p.op("vector", lambda e, vA=vA, dv_=dv_: e.reciprocal(smb[:, 24:26], vA[:, :, dv_]), reads=rpair, writes=[r_smb])
                        p.op("vector", lambda e, vB=vB, dv_=dv_: e.reciprocal(smb[:, 26:28], vB[:, :, dv_]), reads=rpair, wadd=[r_smb])
                        for qs in range(4):
                            p.op("vector", lambda e, qs=qs, vv=vv, g=g: e.tensor_scalar(
                                Ot[:, qs, 512 + g * 64:512 + (g + 1) * 64], vv[qs][:, qs % 2, 0:64], smb[:, 24 + qs:25 + qs], None, ALU.mult),
                                reads=rpair + [r_smb], wadd=[r_Ot])
                for qs in range(4):
                    tt = qb * 4 + qs
                    t0 = tt * 128
                    sl = tt % 2
                    p.dma("sync", xt2[sl][:], x_d[t0:t0 + 128, :], ("xt2", sl), writes=[r_xt2[sl]])
                    pO = PSB[7][:].bitcast(BF16).rearrange("p (k t) -> p k t", k=8)
                    for kc in range(8):
                        p.op("tensor", lambda e, kc=kc, qs=qs: e.transpose(pO[:, kc, :], Ot[:, qs, kc * 128:(kc + 1) * 128], identb[:]),
                             reads=[r_Ot, r_const], writes=[PR[7]] if kc == 0 else (), wadd=[PR[7]] if kc else ())
                    p.op("scalar", lambda e: e.copy(OT[:], pO), reads=[PR[7]], writes=[r_OT])
                    for half in range(2):
                        for kc in range(8):
                            p.op("tensor", lambda e, half=half, kc=kc: e.matmul(
                                PSB[half][:], OT[:, kc, :], wout[:, kc, half * 512:(half + 1) * 512], start=(kc == 0), stop=(kc == 7)),
                                reads=[r_OT, r_wout], writes=[PR[half]] if kc == 0 else (), wadd=[PR[half]] if kc else ())
                    for half in range(2):
                        p.op("vector", lambda e, half=half: e.tensor_tensor(
                            tmpf[:, half * 512:(half + 1) * 512], PSB[half][:], G1[:, half * 512:(half + 1) * 512], ALU.mult),
                            reads=[PR[half], r_cB], writes=[r_tmpf] if half == 0 else (), wadd=[r_tmpf] if half else ())
                    p.op("vector", lambda e, sl=sl: e.tensor_tensor(x1t[sl][:], tmpf[:], xt2[sl][:], ALU.add),
                         reads=[r_tmpf, r_xt2[sl]], writes=[r_x1t[sl]])
                    p.dma("sync", X1d[t0:t0 + 128, :], x1t[sl][:], ("x1st", sl), reads=[r_x1t[sl]], wadd=[r_X1d])
                    p.op("scalar", lambda e, sl=sl: e.activation(tmpf[:], x1t[sl][:], AF.Square, accum_out=smb[:, 32:33]),
                         reads=[r_x1t[sl]], writes=[r_tmpf, r_smb2])
                    p.op("scalar", lambda e: e.activation(smb[:, 33:34], smb[:, 32:33], AF.Sqrt, bias=EPS, scale=1.0 / D),
                         reads=[r_smb2], wadd=[r_smb2])
                    p.op("vector", lambda e: e.reciprocal(smb[:, 34:35], smb[:, 33:34]), reads=[r_smb2], wadd=[r_smb2])
                    p.op("vector", lambda e, sl=sl: e.scalar_tensor_tensor(tmpf[:], x1t[sl][:], smb[:, 34:35], A2[:], ALU.mult, ALU.mult),
                         reads=[r_x1t[sl], r_smb2, r_cB], writes=[r_tmpf])
                    p.op("vector", lambda e: e.tensor_tensor(h2b[:], tmpf[:], B2[:], ALU.add), reads=[r_tmpf, r_cB], writes=[r_h2b])
                    pH = PSB[2][:].bitcast(BF16).rearrange("p (k t) -> p k t", k=8)
                    for kd in range(8):
                        p.op("tensor", lambda e, kd=kd: e.transpose(pH[:, kd, :], h2b[:, kd * 128:(kd + 1) * 128], identb[:]),
                             reads=[r_h2b, r_const], writes=[PR[2]] if kd == 0 else (), wadd=[PR[2]] if kd else ())
                    p.op("scalar", lambda e, sl=sl: e.copy(h2Tst[sl][:], pH), reads=[PR[2]], writes=[r_h2Tst[sl]])
                    p.dma("sync", H2Td[:, :, t0:t0 + 128], h2Tst[sl][:], ("h2st", sl), reads=[r_h2Tst[sl]], wadd=[r_H2Td])
            p.barrier()
        sAB.close()
        if upto == "PB":
            p.emit()
            return nc

        r_UTs, r_Vs = Reg("UTs"), Reg("Vs")

        def make_pc0(stk):
            Ub = [sbuf(stk, "Ub%d" % i, [128, D], BF16) for i in range(2)]
            Vb = [sbuf(stk, "Vb%d" % i, [128, D], BF16) for i in range(2)]
            UTst = [sbuf(stk, "UTst%d" % i, [128, 8, 128], BF16) for i in range(2)]
            r_Ub = [Reg("Ub0"), Reg("Ub1")]
            r_Vb = [Reg("Vb0"), Reg("Vb1")]
            r_UTst = [Reg("UTst0"), Reg("UTst1")]

            def chunk(i):
                sl = i % 2
                p.dma("gpsimd", Ub[sl][:], pu_d[i * 128:(i + 1) * 128, :], ("ub", sl), writes=[r_Ub[sl]])
                p.dma("gpsimd", Vb[sl][:], pv_d[i * 128:(i + 1) * 128, :], ("vb", sl), writes=[r_Vb[sl]])
                pU = PSB[7][:].bitcast(BF16).rearrange("p (k t) -> p k t", k=8)
                for kd in range(8):
                    p.op("tensor", lambda e, kd=kd, sl=sl, pU=pU: e.transpose(pU[:, kd, :], Ub[sl][:, kd * 128:(kd + 1) * 128], identb[:]),
                         reads=[r_Ub[sl], r_const], writes=[PR[7]] if kd == 0 else (), wadd=[PR[7]] if kd else ())
                p.op("scalar", lambda e, sl=sl, pU=pU: e.copy(UTst[sl][:], pU), reads=[PR[7]], writes=[r_UTst[sl]])
                p.dma("sync", UTs[i], UTst[sl][:].rearrange("p k j -> p (k j)"), ("utst", sl), reads=[r_UTst[sl]], wadd=[r_UTs])
                p.dma("sync", Vs[i], Vb[sl][:], ("vst", sl), reads=[r_Vb[sl]], wadd=[r_Vs])
            return chunk

        r_IJGd = Reg("IJGd")
        with ExitStack() as sc1:
            wq = sbuf(sc1, "wq", [128, 8, 2048], BF16)
            SKT = sbuf(sc1, "SKT", [128, 16, 128], BF16)
            skf = [sbuf(sc1, "skf%d" % i, [128, 128], F32) for i in range(2)]
            h2Tb = [sbuf(sc1, "h2Tb%d" % i, [128, 8, 512], BF16) for i in range(2)]
            qT = sbuf(sc1, "qT", [128, 16, 512], BF16)
            Sf = sbuf(sc1, "Sf", [128, 16, 128], F32)
            S2 = sbuf(sc1, "S2", [128, 16, 128], F32)
            mx1 = sbuf(sc1, "mx1", [128, 16, 16], F32)
            ix1 = sbuf(sc1, "ix1", [128, 16, 16], U32)
            ixf = sbuf(sc1, "ixf", [128, 16, 16], F32)
            cand = sbuf(sc1, "cand", [128, 8, 256], F32)
            cand2 = sbuf(sc1, "cand2", [128, 8, 256], F32)
            best = sbuf(sc1, "best", [128, 8, 16], F32)
            pos = sbuf(sc1, "pos", [128, 8, 16], U32)
            posf = sbuf(sc1, "posf", [128, 8, 16], F32)
            thr16 = sbuf(sc1, "thr16", [128, 16], F32)
            k0f = sbuf(sc1, "k0f", [128, 8, 16], F32)
            k1f = sbuf(sc1, "k1f", [128, 8, 16], F32)
            eq = sbuf(sc1, "eq", [128, 8, 16, 16], F32)
            prod = sbuf(sc1, "prod", [128, 8, 16, 16], F32)
            IJGf = sbuf(sc1, "IJGf", [128, 3, 128], F32)
            e_t = sbuf(sc1, "e_t", [128, 8, 16], F32)
            sm1 = sbuf(sc1, "sm1", [128, 16], F32)
            iota16 = sbuf(sc1, "iota16", [128, 16], F32)
            IJGst = [sbuf(sc1, "IJGst%d" % i, [128, 3, 128], F32) for i in range(2)]
            r_wq, r_SKT, r_qT, r_c1 = Reg("wq"), Reg("SKT"), Reg("qT"), Reg("c1")
            r_skf = [Reg("skf0"), Reg("skf1")]
            r_h2Tb = [Reg("h2Tb0"), Reg("h2Tb1")]
            r_Sf = [Reg("Sf%d" % i) for i in range(4)]
            r_S2 = [Reg("S2_%d" % i) for i in range(16)]
            r_mx1 = [Reg("mx1_%d" % i) for i in range(16)]
            r_ix1 = [Reg("ix1_%d" % i) for i in range(16)]
            r_ixf, r_cand, r_k, r_eq, r_prod, r_IJGf, r_et, r_sm1 = (Reg(n) for n in ("ixf", "cand", "k", "eq", "prod", "IJGf", "et", "sm1"))
            r_cand2 = [Reg("cand2_%d" % i) for i in range(8)]
            r_best = [Reg("best%d" % i) for i in range(8)]
            r_pos = [Reg("pos%d" % i) for i in range(8)]
            r_IJGst = [Reg("IJGst0"), Reg("IJGst1")]
            for kd in range(8):
                for hh in range(2):
                    first = (kd == 0 and hh == 0)
                    p.dma("gpsimd", wq[:, kd, hh * 1024:(hh + 1) * 1024], wq_d[kd * 128:(kd + 1) * 128, hh * 1024:(hh + 1) * 1024],
                          ("wq", 0), writes=[r_wq] if first else (), wadd=() if first else [r_wq])
            p.dma("sync", iota16[:], iota16_d, ("c1", 0), writes=[r_c1])
            p.op("vector", lambda e: e.tensor_scalar(thr16[:], iota16[:], 16.0, 16.0, ALU.mult, ALU.add), reads=[r_c1], wadd=[r_c1])
            for ch in range(16):
                sl = ch % 2
                p.dma("sync", skf[sl][:], sk_d[ch], ("skf", sl), writes=[r_skf[sl]])
                p.op("tensor", lambda e, sl=sl: e.transpose(PSB[6 + sl][:, 0:128], skf[sl][:], identf[:]),
                     reads=[r_skf[sl], r_const], writes=[PR[6 + sl]])
                p.op("scalar", lambda e, sl=sl, ch=ch: e.copy(SKT[:, ch, :], PSB[6 + sl][:, 0:128]), reads=[PR[6 + sl]],
                     writes=[r_SKT] if ch == 0 else (), wadd=[r_SKT] if ch else ())
            pc0_chunk = make_pc0(sc1)
            NTB = int(_os.environ.get("NTB", 8))
            for tb in range(NTB):
                slb = tb % 2
                p.dma("sync", h2Tb[slb][:], H2Td[:, :, tb * 512:(tb + 1) * 512], ("h2tb", slb), reads=[r_H2Td] if "PB" in phases else (),
                      writes=[r_h2Tb[slb]])
                for ch in range(16):
                    bk = ch % 2
                    for kd in range(8):
                        p.op("tensor", lambda e, bk=bk, kd=kd, ch=ch, slb=slb: e.matmul(
                            PSB[bk][:], wq[:, kd, ch * 128:(ch + 1) * 128], h2Tb[slb][:, kd, :], start=(kd == 0), stop=(kd == 7)),
                            reads=[r_wq, r_h2Tb[slb]], writes=[PR[bk]] if kd == 0 else (), wadd=[PR[bk]] if kd else ())
                    p.op("scalar", lambda e, bk=bk, ch=ch: e.copy(qT[:, ch, :], PSB[bk][:]), reads=[PR[bk]],
                         writes=[r_qT] if ch == 0 else (), wadd=[r_qT] if ch else ())
                for ts in range(4):
                    tt = tb * 4 + ts
                    t0 = tt * 128
                    sl = tt % 2
                    for k4 in range(4):
                        pc0_chunk(tt * 4 + k4)
                    for ch in range(16):
                        bk = 2 + ch // 4
                        p.op("tensor", lambda e, bk=bk, ch=ch, ts=ts: e.matmul(
                            PSB[bk][:, (ch % 4) * 128:(ch % 4 + 1) * 128], qT[:, ch, ts * 128:(ts + 1) * 128], SKT[:, ch, :],
                            start=True, stop=True, skip_group_check=True),
                            reads=[r_qT, r_SKT], writes=[PR[bk]] if ch % 4 == 0 else (), wadd=[PR[bk]] if ch % 4 else ())
                    for b4 in range(4):
                        p.op("scalar", lambda e, b4=b4: e.copy(Sf[:, b4 * 4:(b4 + 1) * 4, :], PSB[2 + b4][:].rearrange("p (c n) -> p c n", c=4)),
                             reads=[PR[2 + b4]], writes=[r_Sf[b4]])
                    for g in range(16):
                        p.op("vector", lambda e, g=g: e.max(mx1[:, g, 0:8], Sf[:, g, :]), reads=[r_Sf[g // 4]], writes=[r_mx1[g]], soft=True)
                    for g in range(16):
                        p.op("vector", lambda e, g=g: e.max_index(ix1[:, g, 0:8], mx1[:, g, 0:8], Sf[:, g, :]),
                             reads=[r_Sf[g // 4], r_mx1[g]], writes=[r_ix1[g]], soft=True)
                    for g in range(16):
                        p.op("vector", lambda e, g=g: e.match_replace(S2[:, g, :], mx1[:, g, 0:8], Sf[:, g, :], -1e30),
                             reads=[r_Sf[g // 4], r_mx1[g]], writes=[r_S2[g]], soft=True)
                    for g in range(16):
                        p.op("vector", lambda e, g=g: e.max(mx1[:, g, 8:16], S2[:, g, :]), reads=[r_S2[g]], wadd=[r_mx1[g]], soft=True)
                    for g in range(16):
                        p.op("vector", lambda e, g=g: e.max_index(ix1[:, g, 8:16], mx1[:, g, 8:16], S2[:, g, :]),
                             reads=[r_S2[g], r_mx1[g]], wadd=[r_ix1[g]], soft=True)
                    p.op("vector", lambda e: e.tensor_copy(ixf[:], ix1[:]), reads=r_ix1, writes=[r_ixf])
                    mxv = mx1[:].rearrange("p (h two) k -> p h two k", two=2)
                    p.op("vector", lambda e, mxv=mxv: e.tensor_tensor(
                        cand[:].rearrange("p h (a b) -> p h a b", a=16),
                        mxv[:, :, 0, :].unsqueeze(3).broadcast_to([128, 8, 16, 16]),
                        mxv[:, :, 1, :].unsqueeze(2).broadcast_to([128, 8, 16, 16]), ALU.add),
                        reads=r_mx1, writes=[r_cand])
                    for h in range(8):
                        p.op("vector", lambda e, h=h: e.max(best[:, h, 0:8], cand[:, h, :]), reads=[r_cand], writes=[r_best[h]], soft=True)
                    for h in range(8):
                        p.op("vector", lambda e, h=h: e.max_index(pos[:, h, 0:8], best[:, h, 0:8], cand[:, h, :]),
                             reads=[r_cand, r_best[h]], writes=[r_pos[h]], soft=True)
                    for h in range(8):
                        p.op("vector", lambda e, h=h: e.match_replace(cand2[:, h, :], best[:, h, 0:8], cand[:, h, :], -1e30),
                             reads=[r_cand, r_best[h]], writes=[r_cand2[h]], soft=True)
                    for h in range(8):
                        p.op("vector", lambda e, h=h: e.max(best[:, h, 8:16], cand2[:, h, :]), reads=[r_cand2[h]], wadd=[r_best[h]], soft=True)
                    for h in range(8):
                        p.op("vector", lambda e, h=h: e.max_index(pos[:, h, 8:16], best[:, h, 8:16], cand2[:, h, :]),
                             reads=[r_cand2[h], r_best[h]], wadd=[r_pos[h]], soft=True)
                    p.op("vector", lambda e: e.tensor_copy(posf[:], pos[:]), reads=r_pos, writes=[r_k])
                    p.op("vector", lambda e: e.tensor_tensor(
                        eq[:], posf[:].unsqueeze(3).broadcast_to([128, 8, 16, 16]),
                        thr16[:].unsqueeze(1).unsqueeze(1).broadcast_to([128, 8, 16, 16]), ALU.is_ge),
                        reads=[r_k, r_c1], writes=[r_eq])
                    p.op("vector", lambda e: e.tensor_reduce(k0f[:], eq[:], AX.X, ALU.add), reads=[r_eq], wadd=[r_k])
                    p.op("vector", lambda e: e.scalar_tensor_tensor(k1f[:], k0f[:], -16.0, posf[:], ALU.mult, ALU.add),
                         reads=[r_k], wadd=[r_k])
                    ixv = ixf[:].rearrange("p (h two) k -> p h two k", two=2)
                    io4 = iota16[:].unsqueeze(1).unsqueeze(1).broadcast_to([128, 8, 16, 16])
                    for which, kf in ((0, k0f), (1, k1f)):
                        p.op("vector", lambda e, kf=kf: e.tensor_tensor(
                            eq[:], kf[:].unsqueeze(3).broadcast_to([128, 8, 16, 16]), io4, ALU.is_equal),
                            reads=[r_k, r_c1], writes=[r_eq])
                        p.op("vector", lambda e, which=which, ixv=ixv: e.tensor_tensor(
                            prod[:], eq[:], ixv[:, :, which, :].unsqueeze(2).broadcast_to([128, 8, 16, 16]), ALU.mult),
                            reads=[r_eq, r_ixf], writes=[r_prod])
                        p.op("vector", lambda e, which=which: e.tensor_reduce(
                            IJGf[:, which, :].rearrange("p (h s) -> p h s", h=8), prod[:], AX.X, ALU.add),
                            reads=[r_prod], writes=[r_IJGf] if which == 0 else (), wadd=[r_IJGf] if which else ())
                    p.op("vector", lambda e: e.tensor_tensor(e_t[:], best[:], best[:, :, 0:1].broadcast_to([128, 8, 16]), ALU.subtract),
                         reads=r_best, writes=[r_et])
                    p.op("scalar", lambda e: e.activation(e_t[:], e_t[:], AF.Exp), reads=[r_et], writes=[r_et])
                    p.op("vector", lambda e: e.tensor_reduce(sm1[:, 0:8], e_t[:], AX.X, ALU.add), reads=[r_et], writes=[r_sm1])
                    p.op("vector", lambda e: e.reciprocal(sm1[:, 8:16], sm1[:, 0:8]), reads=[r_sm1], wadd=[r_sm1])
                    p.op("vector", lambda e: e.tensor_tensor(
                        IJGf[:, 2, :].rearrange("p (h s) -> p h s", h=8), e_t[:],
                        sm1[:, 8:16].unsqueeze(2).broadcast_to([128, 8, 16]), ALU.mult),
                        reads=[r_et, r_sm1], wadd=[r_IJGf])
                    for k in range(3):
                        p.op("tensor", lambda e, k=k: e.transpose(PSB[6][:, k * 128:(k + 1) * 128], IJGf[:, k, :], identf[:]),
                             reads=[r_IJGf, r_const], writes=[PR[6]] if k == 0 else (), wadd=[PR[6]] if k else ())
                    p.op("scalar", lambda e, sl=sl: e.copy(IJGst[sl][:], PSB[6][:, 0:384].rearrange("p (k t) -> p k t", k=3)),
                         reads=[PR[6]], writes=[r_IJGst[sl]])
                    p.dma("sync", IJGd[:, :, t0:t0 + 128], IJGst[sl][:], ("ijgst", sl), reads=[r_IJGst[sl]], wadd=[r_IJGd])
            if UVdbg is not None:
                dbt = sbuf(sc1, "dbt", [128, D], BF16)
                r_dbt = Reg("dbt")
                for k, (src, idx) in enumerate(((UTs, 0), (UTs, 77), (UTs, 127), (Vs, 5))):
                    p.dma("sync", dbt[:], src[idx], ("dbg", 2), reads=[r_UTs, r_Vs], writes=[r_dbt])
                    p.dma("sync", UVdbg[k], dbt[:], ("dbg", 3), reads=[r_dbt])
            p.barrier()
        if upto == "PC1":
            p.emit()
            return nc

        with ExitStack() as sc2:
            GT = [sbuf(sc2, "GT%d" % i, [128, 128, 256], BF16) for i in range(2)]
            GSZ = 2
            Ust = [sbuf(sc2, "Ust%d" % i, [128, GSZ, D], BF16) for i in range(2)]
            Vst = [sbuf(sc2, "Vst%d" % i, [128, GSZ, D], BF16) for i in range(2)]
            IJGt = [sbuf(sc2, "IJGt%d" % i, [128, 3, 256], F32) for i in range(2)]
            h2Tt = [sbuf(sc2, "h2Tt%d" % i, [128, 8, 256], BF16) for i in range(2)]
            x1t2 = sbuf(sc2, "x1t2", [128, 2, D], F32)
            OJ = [sbuf(sc2, "OJ%d" % i, [128, 128], BF16) for i in range(4)]
            OIg = [sbuf(sc2, "OIg%d" % i, [128, 128], BF16) for i in range(4)]
            gA = [sbuf(sc2, "gA%d" % i, [128, 256], BF16) for i in range(2)]
            Wt = [sbuf(sc2, "Wt%d" % i, [128, 256], BF16) for i in range(2)]
            G2t = sbuf(sc2, "G2t", [128, D], F32)
            Gft = sbuf(sc2, "Gft", [128, D], F32)
            tmp2 = sbuf(sc2, "tmp2", [128, D], F32)
            outt = sbuf(sc2, "outt", [128, D], F32)
            iotab = sbuf(sc2, "iotab", [128, 128], BF16)
            smc = sbuf(sc2, "smc", [128, 8], F32)
            r_x1t2, r_c2, r_tmp2, r_smc, r_outt = (Reg(n) for n in ("x1t2", "c2", "tmp2", "smc", "outt"))
            r_GT = [Reg("GT0"), Reg("GT1")]
            r_Ust, r_Vst = [Reg("Ust0"), Reg("Ust1")], [Reg("Vst0"), Reg("Vst1")]
            r_IJGt, r_h2Tt = [Reg("IJGt0"), Reg("IJGt1")], [Reg("h2Tt0"), Reg("h2Tt1")]
            r_OJ = [Reg("OJ%d" % i) for i in range(4)]
            r_OIg = [Reg("OIg%d" % i) for i in range(4)]
            r_gA, r_Wt = [Reg("gA0"), Reg("gA1")], [Reg("Wt0"), Reg("Wt1")]
            r_out = Reg("out")
            p.dma("sync", iotab[:], iotab_d, ("c2", 0), writes=[r_c2])
            bcast_row("sync", G2t[:], modrows_d[7:8, :], ("c2", 1), wadd=[r_c2])
            bcast_row("sync", Gft[:], modrows_d[8:9, :], ("c2", 2), wadd=[r_c2])
            NTL = int(_os.environ.get("NTL", 16))
            NI = 128
            NG = NI // GSZ

            def load_ijg(T):
                s_ = T % 2
                p.dma("sync", IJGt[s_][:], IJGd[:, :, T * 256:(T + 1) * 256], ("ijgt", s_), reads=[r_IJGd], writes=[r_IJGt[s_]])

            def gbuild(T, c):
                s_ = T % 2
                s4 = c % 4
                bk = 6 + (c // 4) % 2
                p.op("vector", lambda e: e.tensor_scalar(OJ[s4][:], iotab[:], IJGt[s_][:, 1, c:c + 1], None, ALU.is_equal),
                     reads=[r_IJGt[s_], r_c2], writes=[r_OJ[s4]])
                p.op("vector", lambda e: e.tensor_scalar(OIg[s4][:], iotab[:], IJGt[s_][:, 0, c:c + 1], IJGt[s_][:, 2, c:c + 1],
                                                         ALU.is_equal, ALU.mult),
                     reads=[r_IJGt[s_], r_c2], writes=[r_OIg[s4]])
                p.op("tensor", lambda e: e.matmul(PSB[bk][:, (c % 4) * 128:(c % 4 + 1) * 128], OJ[s4][:], OIg[s4][:],
                                                 start=True, stop=True, skip_group_check=True),
                     reads=[r_OJ[s4], r_OIg[s4]], writes=[PR[bk]] if c % 4 == 0 else (), wadd=[PR[bk]] if c % 4 else ())
                if c % 4 == 3:
                    cb = c - 3
                    p.op("scalar", lambda e: e.copy(
                        GT[s_][:, :, cb:cb + 4].rearrange("p i c -> p c i"), PSB[bk][:].rearrange("p (c i) -> p c i", c=4)),
                        reads=[PR[bk]], writes=[r_GT[s_]] if c == 3 else (), wadd=[r_GT[s_]] if c != 3 else ())

            load_ijg(0)
            for c in range(256):
                gbuild(0, c)
            for T in range(NTL):
                s = T % 2
                c00 = T * 256
                p.dma("sync", h2Tt[s][:], H2Td[:, :, c00:c00 + 256], ("h2tt", s), writes=[r_h2Tt[s]])
                if T + 1 < NTL:
                    load_ijg(T + 1)

                def stream(ig):
                    sg = ig % 2
                    p.dma("sync", Ust[sg][:], UTs[ig * GSZ:(ig + 1) * GSZ].rearrange("g p d -> p g d"), ("ust", sg),
                          reads=[r_UTs], writes=[r_Ust[sg]])
                    p.dma("sync", Vst[sg][:], Vs[ig * GSZ:(ig + 1) * GSZ].rearrange("g p d -> p g d"), ("vstr", sg),
                          reads=[r_Vs], writes=[r_Vst[sg]])

                def emit_A(i, s=s):
                    ig, ii = i // GSZ, i % GSZ
                    sg = ig % 2
                    ab = i % 2
                    for kd in range(8):
                        p.op("tensor", lambda e, kd=kd: e.matmul(
                            PSB[ab][:, 0:256], Ust[sg][:, ii, kd * 128:(kd + 1) * 128], h2Tt[s][:, kd, :], start=(kd == 0), stop=(kd == 7)),
                            reads=[r_Ust[sg], r_h2Tt[s]], writes=[PR[ab]] if kd == 0 else (), wadd=[PR[ab]] if kd else ())
                    p.op("scalar", lambda e: e.activation(gA[ab][:], PSB[ab][:, 0:256], AF.Gelu), reads=[PR[ab]], writes=[r_gA[ab]])
                    p.op("vector", lambda e: e.tensor_tensor(Wt[ab][:], gA[ab][:], GT[s][:, i, :], ALU.mult),
                         reads=[r_gA[ab], r_GT[s]], writes=[r_Wt[ab]])

                def emit_out(i):
                    ig, ii = i // GSZ, i % GSZ
                    sg = ig % 2
                    ab = i % 2
                    for sub in range(2):
                        for half in range(2):
                            bk = 2 + sub * 2 + half
                            p.op("tensor", lambda e, bk=bk, sub=sub, half=half: e.matmul(
                                PSB[bk][:], Wt[ab][:, sub * 128:(sub + 1) * 128], Vst[sg][:, ii, half * 512:(half + 1) * 512],
                                start=(i == 0), stop=(i == NI - 1)),
                                reads=[r_Wt[ab], r_Vst[sg]], writes=[PR[bk]] if i == 0 else (), wadd=[PR[bk]] if i else ())

                stream(0)
                emit_A(0)
                for i in range(NI):
                    if i % GSZ == 0 and i // GSZ + 1 < NG:
                        stream(i // GSZ + 1)
                    if i + 1 < NI:
                        emit_A(i + 1)
                    emit_out(i)
                    if T + 1 < NTL:
                        gbuild(T + 1, 2 * i)
                        gbuild(T + 1, 2 * i + 1)
                p.dma("sync", x1t2[:], X1d[c00:c00 + 256, :].rearrange("(s p) d -> p s d", p=128), ("x1t2", 0), writes=[r_x1t2])
                for sub in range(2):
                    tt = T * 2 + sub
                    t0 = tt * 128
                    for half in range(2):
                        bk = 2 + sub * 2 + half
                        p.op("vector", lambda e, bk=bk, half=half: e.tensor_tensor(
                            tmp2[:, half * 512:(half + 1) * 512], PSB[bk][:], G2t[:, half * 512:(half + 1) * 512], ALU.mult),
                            reads=[PR[bk], r_c2], writes=[r_tmp2] if half == 0 else (), wadd=[r_tmp2] if half else ())
                    p.op("vector", lambda e, sub=sub: e.tensor_tensor(tmp2[:], tmp2[:], x1t2[:, sub, :], ALU.add),
                         reads=[r_tmp2, r_x1t2], writes=[r_tmp2])
                    p.op("scalar", lambda e: e.activation(outt[:], tmp2[:], AF.Square, accum_out=smc[:, 0:1]), reads=[r_tmp2],
                         writes=[r_outt, r_smc])
                    p.op("scalar", lambda e: e.activation(smc[:, 1:2], smc[:, 0:1], AF.Sqrt, bias=EPS, scale=1.0 / D), reads=[r_smc], wadd=[r_smc])
                    p.op("vector", lambda e: e.reciprocal(smc[:, 2:3], smc[:, 1:2]), reads=[r_smc], wadd=[r_smc])
                    p.op("vector", lambda e: e.scalar_tensor_tensor(outt[:], tmp2[:], smc[:, 2:3], Gft[:], ALU.mult, ALU.mult),
                         reads=[r_tmp2, r_smc, r_c2], writes=[r_outt])
                    p.dma("sync", out_d[t0:t0 + 128, :], outt[:], ("outst", 0), reads=[r_outt], wadd=[r_out])
            p.barrier()
        p.emit()
        return nc


def _host_inputs(inputs):
    f = lambda k: np.ascontiguousarray(np.asarray(inputs[k], dtype=np.float32))
    x, c, ctx, c_ctx = f("x"), f("c"), f("ctx"), f("c_ctx")
    vecs = np.zeros((1, 8 * D), np.float32)
    vecs[0, 0:D] = f("norm1_g")[0]
    vecs[0, D:2 * D] = f("norm2_g")[0]
    vecs[0, 2 * D:3 * D] = f("final_norm_g")
    vecs[0, 3 * D:3 * D + 128] = f("diff_subln_g")[0]
    vecs[0, 4 * D:4 * D + 64] = f("gqa_q_norm_g")[0]
    vecs[0, 5 * D:5 * D + 64] = f("gqa_k_norm_g")[0]
    vecs[0, 6 * D:6 * D + 256] = np.concatenate([f("diff_lq1")[0], f("diff_lk1")[0], f("diff_lq2")[0], f("diff_lk2")[0]])
    C, Sg = _rope_tables()
    shared = {
        "w_mod": f("w_mod")[0], "b_mod": f("b_mod")[0][None, :], "vecs": vecs,
        "w_in": f("w_in")[0], "w_out": f("w_out")[0], "peer_wq": f("peer_wq")[0],
        "subkeys": np.ascontiguousarray(f("peer_subkeys")[0].reshape(16, 128, 128)),
        "peer_u": f("peer_u")[0], "peer_v": f("peer_v")[0],
        "ropec": C, "ropes": Sg,
        "identb": np.eye(128, dtype=np.float32).astype(ml_dtypes.bfloat16),
        "identf": np.eye(128, dtype=np.float32),
        "iotab": np.tile(np.arange(128, dtype=np.float32)[None, :], (128, 1)).astype(ml_dtypes.bfloat16),
        "iota16": np.tile(np.arange(16, dtype=np.float32)[None, :], (128, 1)),
    }
    maps = []
    for b in range(8):
        cc = np.stack([c[b].reshape(8, 128).T, c_ctx.reshape(8, 128).T], axis=-1)
        m = dict(shared)
        m["x"] = x[b]
        m["ctx"] = ctx[b]
        m["cc"] = np.ascontiguousarray(cc.astype(np.float32))
        maps.append(m)
    return maps


def kernel(**inputs):
    maps = _host_inputs(inputs)
    nc = build()
    maps = [{k: m[k] for k in nc._used_inputs} for m in maps]
    res = run_bass_kernel_spmd(nc, maps, core_ids=list(range(8)))
    return np.stack([np.asarray(r["out"], dtype=np.float32) for r in res.results], axis=0)
```

```python
import os as _os
import numpy as np
from contextlib import ExitStack
import ml_dtypes
import concourse.bass as bass
import concourse.mybir as mybir
from concourse.bass_utils import run_bass_kernel_spmd

F32 = mybir.dt.float32
BF16 = mybir.dt.bfloat16
U32 = mybir.dt.uint32
AF = mybir.ActivationFunctionType
ALU = mybir.AluOpType
AX = mybir.AxisListType
ENGS = ["sync", "scalar", "vector", "gpsimd", "tensor"]
EPS = 1e-6
S = 4096
NT = 32
CTXL = 256
NKT = 34
LK = 4352
D = 1024


class Reg:
    __slots__ = ("name", "w", "r", "gdeps")

    def __init__(self, name):
        self.name = name
        self.w = []
        self.r = []
        self.gdeps = []


class Op:
    __slots__ = ("eng", "fn", "deps", "sig", "val", "key", "inc", "semkey", "seq")


class Prog:
    def __init__(self, nc, stack):
        self.nc = nc
        self.stack = stack
        self.q = {e: [] for e in ENGS}
        self.sems = {}
        self.last = {e: None for e in ENGS}
        self.dmas = []
        self.nseq = 0

    def op(self, eng, fn, reads=(), writes=(), wadd=(), after=(), semkey=None, soft=False, inc=1):
        o = Op()
        o.eng, o.fn, o.sig, o.val, o.key, o.inc, o.semkey = eng, fn, False, None, None, inc, semkey
        o.seq = self.nseq
        self.nseq += 1
        deps = []
        for r in reads:
            deps.extend(r.w)
        for r in writes:
            g = list(r.w) + list(r.r)
            r.gdeps = g
            deps.extend(g)
        for r in wadd:
            deps.extend(r.gdeps)
            deps.extend(r.r)
        deps.extend(after)
        seen = set()
        o.deps = []
        latest = {}
        for d in deps:
            if d is None or id(d) in seen:
                continue
            seen.add(id(d))
            if d.semkey is None:
                if d.eng == eng and (eng == "tensor" or soft):
                    continue
                if d.eng not in latest or latest[d.eng].seq < d.seq:
                    latest[d.eng] = d
                continue
            d.sig = True
            o.deps.append(d)
        for d in latest.values():
            d.sig = True
            o.deps.append(d)
        for r in reads:
            if semkey is None:
                r.r = [x for x in r.r if not (x.eng == eng and x.semkey is None)]
            r.r.append(o)
        for r in writes:
            r.w = [o]
            r.r = []
        for r in wadd:
            if semkey is None:
                r.w = [x for x in r.w if not (x.eng == eng and x.semkey is None)]
            r.w.append(o)
        self.q[eng].append(o)
        self.last[eng] = o
        if semkey is not None:
            self.dmas.append(o)
        return o

    def dma(self, eng, out, in_, semkey, reads=(), writes=(), wadd=(), after=(), **kw):
        return self.op(eng, lambda e: e.dma_start(out=out, in_=in_, **kw), reads, writes, wadd, after,
                       semkey=semkey, inc=16)

    def barrier(self):
        lasts = [self.last[e] for e in ENGS if self.last[e] is not None]
        pend = list(self.dmas)
        self.dmas = []
        st1 = [self.op(e, lambda en: en.nop(), after=lasts + pend) for e in ENGS]
        for e in ENGS:
            self.op(e, lambda en: en.nop(), after=st1)

    def emit(self):
        nc = self.nc
        cnt = {}
        for en in ENGS:
            for o in self.q[en]:
                if o.sig:
                    key = o.semkey if o.semkey is not None else ("E", en)
                    if key not in self.sems:
                        self.sems[key] = self.stack.enter_context(nc.semaphore("s%d" % len(self.sems)))
                    cnt[key] = cnt.get(key, 0) + o.inc
                    o.key, o.val = key, cnt[key]
        with nc.Block() as block:
            for en in ENGS:
                q = self.q[en]
                if not q:
                    continue

                def body(e, q=q):
                    waited = {}
                    for o in q:
                        need = {}
                        for d in o.deps:
                            if waited.get(d.key, 0) >= d.val:
                                continue
                            need[d.key] = max(need.get(d.key, 0), d.val)
                        for key, val in need.items():
                            waited[key] = val
                            e.wait_ge(self.sems[key], val)
                        ins = o.fn(e)
                        if o.sig:
                            ins.then_inc(self.sems[o.key], o.inc)

                getattr(block, en)(body)


def _rope_tables():
    half = 16
    freqs = (np.float32(10000.0) ** (-np.arange(half, dtype=np.float32) / np.float32(half))).astype(np.float32)
    t = np.arange(S)
    row = (t // 64).astype(np.float32)
    col = (t % 64).astype(np.float32)
    C = np.zeros((S, 64), np.float32)
    Sg = np.zeros((S, 64), np.float32)
    for a, pos in enumerate((row, col)):
        ang = (pos[:, None] * freqs[None, :]).astype(np.float32)
        c = np.cos(ang).astype(np.float32)
        s = np.sin(ang).astype(np.float32)
        C[:, a * 32:a * 32 + 16] = c
        C[:, a * 32 + 16:a * 32 + 32] = c
        Sg[:, a * 32:a * 32 + 16] = -s
        Sg[:, a * 32 + 16:a * 32 + 32] = s
    return C, Sg


def build(upto="PC2", dbg=()):
    nc = bass.Bass("TRN2", target_bir_lowering=False)
    phases = ["P0", "PA", "PB", "PC0", "PC1", "PC2"]
    phases = phases[:phases.index(upto) + 1]
    first_use = {"w_out": "PB", "peer_wq": "PC1", "subkeys": "PC1", "peer_u": "PC0", "peer_v": "PC0",
                 "iotab": "PC1", "iota16": "PC1", "w_in": "PA", "ropec": "PA", "ropes": "PA", "x": "PA", "ctx": "PA"}
    used = []

    def dt_in(name, shape, dt=F32):
        if first_use.get(name, "P0") not in phases:
            return None
        used.append(name)
        return nc.dram_tensor(name, shape, dt, kind="ExternalInput").ap()

    def scratch(name, shape, dt):
        kind = "ExternalOutput" if name in dbg else "Internal"
        return nc.dram_tensor(name, shape, dt, kind=kind).ap()

    x_d = dt_in("x", [S, D])
    ctx_d = dt_in("ctx", [CTXL, D])
    cc_d = dt_in("cc", [128, 8, 2])
    wmod_d = dt_in("w_mod", [D, 6 * D])
    bmod_d = dt_in("b_mod", [1, 6 * D])
    vecs_d = dt_in("vecs", [1, 8 * D])
    win_d = dt_in("w_in", [D, 2304])
    wout_d = dt_in("w_out", [D, D])
    wq_d = dt_in("peer_wq", [D, 2048])
    sk_d = dt_in("subkeys", [16, 128, 128])
    pu_d = dt_in("peer_u", [16384, D])
    pv_d = dt_in("peer_v", [16384, D])
    cos_d = dt_in("ropec", [S, 64])
    sin_d = dt_in("ropes", [S, 64])
    identb_d = dt_in("identb", [128, 128], BF16)
    identf_d = dt_in("identf", [128, 128])
    iotab_d = dt_in("iotab", [128, 128], BF16)
    iota16_d = dt_in("iota16", [128, 16])
    out_d = nc.dram_tensor("out", [S, D], F32, kind="ExternalOutput").ap() if "PC2" in phases else None

    modrows_d = scratch("modrows", [12, D], F32)
    QTd = scratch("QTd", [128, 16, S], BF16)
    X1d = scratch("X1d", [S, D], F32)
    H2Td = scratch("H2Td", [128, 8, S], BF16)
    IJGd = scratch("IJGd", [128, 3, S], F32)
    UTs = scratch("UTs", [128, 128, D], BF16)
    Vs = scratch("Vs", [128, 128, D], BF16)
    UVdbg = scratch("UVdbg", [4, 128, D], BF16) if "UVdbg" in dbg else None
    KTdbg = scratch("KTdbg", [128, 6, LK], BF16) if "KTdbg" in dbg else None
    Vdbg = scratch("Vdbg", [128, NKT, 646], BF16) if "Vdbg" in dbg else None

    nc._used_inputs = used

    with ExitStack() as st:
        p = Prog(nc, st)
        PSB = [st.enter_context(nc.psum_tensor("psb%d" % i, [128, 512], F32)) for i in range(8)]
        PR = [Reg("ps%d" % i) for i in range(8)]

        def sbuf(stack, name, shape, dt):
            return stack.enter_context(nc.sbuf_tensor("sb_" + name, shape, dt))

        identb = sbuf(st, "identb", [128, 128], BF16)
        identf = sbuf(st, "identf", [128, 128], F32)
        r_const = Reg("const")
        p.dma("sync", identb[:], identb_d, ("c", 0), writes=[r_const])
        p.dma("sync", identf[:], identf_d, ("c", 1), wadd=[r_const])

        def bcast_row(eng, dst, row, semkey, writes=(), wadd=()):
            return p.dma(eng, dst, row.partition_broadcast(128), semkey, writes=writes, wadd=wadd)

        if "P0" in phases:
            with ExitStack() as s0:
                cc = sbuf(s0, "cc", [128, 8, 2], F32)
                sc = sbuf(s0, "sc", [128, 8, 2], F32)
                wm = [sbuf(s0, "wm%d" % i, [128, 2048], F32) for i in range(2)]
                bmod = sbuf(s0, "bmod", [1, 6 * D], F32)
                vecs = sbuf(s0, "vecs", [1, 8 * D], F32)
                modx = sbuf(s0, "modx", [1, 6 * D], F32)
                modc = sbuf(s0, "modc", [1, 2 * D], F32)
                rows = sbuf(s0, "rows", [1, 12, D], F32)
                sm = sbuf(s0, "sm", [1, 16], F32)
                t64 = sbuf(s0, "t64", [1, 128], F32)
                r_cc, r_sc, r_bm, r_vec = Reg("cc"), Reg("sc"), Reg("bm"), Reg("vec")
                r_wm = [Reg("wm0"), Reg("wm1")]
                r_modx, r_modc, r_rows, r_sm, r_t64 = Reg("modx"), Reg("modc"), Reg("rows"), Reg("sm"), Reg("t64")
                p.dma("sync", cc[:], cc_d, ("p0", 0), writes=[r_cc])
                p.dma("sync", bmod[:], bmod_d, ("p0", 1), writes=[r_bm])
                p.dma("sync", vecs[:], vecs_d, ("p0", 2), writes=[r_vec])
                p.op("scalar", lambda e: e.activation(sc[:], cc[:], AF.Silu), reads=[r_cc], writes=[r_sc])
                it = 0
                for gi in range(3):
                    for kd in range(8):
                        sl = it % 2
                        it += 1
                        p.dma("sync", wm[sl][:], wmod_d[kd * 128:(kd + 1) * 128, gi * 2048:(gi + 1) * 2048],
                              ("wm", sl), writes=[r_wm[sl]])
                        for cb in range(4):
                            p.op("tensor", lambda e, sl=sl, kd=kd, cb=cb: e.matmul(
                                PSB[cb][0:1, :], sc[:, kd, 0:1], wm[sl][:, cb * 512:(cb + 1) * 512],
                                start=(kd == 0), stop=(kd == 7)),
                                reads=[r_sc, r_wm[sl]], writes=[PR[cb]] if kd == 0 else (), wadd=[PR[cb]] if kd else ())
                            if gi == 0:
                                p.op("tensor", lambda e, sl=sl, kd=kd, cb=cb: e.matmul(
                                    PSB[4 + cb][0:1, :], sc[:, kd, 1:2], wm[sl][:, cb * 512:(cb + 1) * 512],
                                    start=(kd == 0), stop=(kd == 7)),
                                    reads=[r_sc, r_wm[sl]], writes=[PR[4 + cb]] if kd == 0 else (),
                                    wadd=[PR[4 + cb]] if kd else ())
                    for cb in range(4):
                        c0 = gi * 2048 + cb * 512
                        p.op("vector", lambda e, cb=cb, c0=c0: e.tensor_tensor(
                            modx[0:1, c0:c0 + 512], PSB[cb][0:1, :], bmod[0:1, c0:c0 + 512], ALU.add),
                            reads=[PR[cb], r_bm], wadd=[r_modx])
                        if gi == 0:
                            p.op("vector", lambda e, cb=cb, c0=c0: e.tensor_tensor(
                                modc[0:1, c0:c0 + 512], PSB[4 + cb][0:1, :], bmod[0:1, c0:c0 + 512], ALU.add),
                                reads=[PR[4 + cb], r_bm], wadd=[r_modc])
                V = lambda k: vecs[0:1, k * D:(k + 1) * D]
                MX = lambda k: modx[0:1, k * D:(k + 1) * D]
                MC = lambda k: modc[0:1, k * D:(k + 1) * D]
                RW = lambda k: rows[0:1, k, :]

                def stt(out, in0, in1):
                    p.op("vector", lambda e: e.scalar_tensor_tensor(out, in0, 1.0, in1, ALU.add, ALU.mult),
                         reads=[r_modx, r_modc, r_vec], wadd=[r_rows])

                def cp(out, in_):
                    p.op("vector", lambda e: e.tensor_copy(out, in_), reads=[r_modx, r_modc, r_vec, r_sm],
                         wadd=[r_rows])

                stt(RW(0), MX(1), V(0))
                cp(RW(1), MX(0))
                stt(RW(2), MC(1), V(0))
                cp(RW(3), MC(0))
                cp(RW(4), MX(2))
                stt(RW(5), MX(4), V(1))
                cp(RW(6), MX(3))
                cp(RW(7), MX(5))
                cp(RW(8), V(2))
                L = vecs[0:1, 6 * D:6 * D + 256]
                p.op("vector", lambda e: e.tensor_tensor(t64[0:1, 0:64], L[:, 0:64], L[:, 64:128], ALU.mult),
                     reads=[r_vec], writes=[r_t64])
                p.op("vector", lambda e: e.tensor_tensor(t64[0:1, 64:128], L[:, 128:192], L[:, 192:256], ALU.mult),
                     reads=[r_vec], wadd=[r_t64])
                p.op("vector", lambda e: e.tensor_reduce(sm[0:1, 0:2], t64[0:1, :].rearrange("p (a b) -> p a b", a=2),
                                                         AX.X, ALU.add), reads=[r_t64], writes=[r_sm])
                p.op("scalar", lambda e: e.activation(sm[0:1, 2:4], sm[0:1, 0:2], AF.Exp), reads=[r_sm], wadd=[r_sm])
                p.op("vector", lambda e: e.tensor_tensor(sm[0:1, 4:5], sm[0:1, 2:3], sm[0:1, 3:4], ALU.subtract),
                     reads=[r_sm], wadd=[r_sm])
                p.op("vector", lambda e: e.memset(rows[0:1, 9, :], 0.0), wadd=[r_rows])
                p.op("vector", lambda e: e.tensor_scalar(rows[0:1, 9, 0:128], t64[0:1, :], 0.0, None, ALU.mult),
                     reads=[r_t64], wadd=[r_rows])
                p.op("vector", lambda e: e.tensor_scalar(sm[0:1, 5:6], sm[0:1, 4:5], -1.0, -0.2, ALU.mult, ALU.add),
                     reads=[r_sm], wadd=[r_sm])
                p.op("vector", lambda e: e.tensor_scalar(rows[0:1, 9, 0:128], rows[0:1, 9, 0:128], sm[0:1, 5:6], None,
                                                         ALU.add), reads=[r_sm, r_rows], wadd=[r_rows])
                p.op("vector", lambda e: e.tensor_scalar(rows[0:1, 9, 128:256], vecs[0:1, 3 * D:3 * D + 128], 0.8, None,
                                                         ALU.mult), reads=[r_vec], wadd=[r_rows])
                cp(rows[0:1, 9, 256:320], vecs[0:1, 4 * D:4 * D + 64])
                cp(rows[0:1, 9, 320:384], vecs[0:1, 5 * D:5 * D + 64])
                p.op("vector", lambda e: e.memset(rows[0:1, 10:12, :], 0.0), wadd=[r_rows])
                r_mrd = Reg("modrows_d")
                p.dma("sync", modrows_d.rearrange("(o r) d -> o r d", o=1), rows[:], ("p0", 3), reads=[r_rows],
                      writes=[r_mrd])
                p.barrier()
        if upto == "P0":
            p.emit()
            return nc

        sAB = st.enter_context(ExitStack())
        KT = sbuf(sAB, "KT", [128, 5, LK], BF16)
        Vsb = sbuf(sAB, "Vsb", [128, NKT, 646], BF16)
        r_KT = [Reg("KT%d" % n) for n in range(NKT)]
        r_V = [Reg("V%d" % n) for n in range(NKT)]

        with ExitStack() as sa:
            win = sbuf(sa, "win", [128, 8, 2304], BF16)
            xt = [sbuf(sa, "xt%d" % i, [128, D], F32) for i in range(2)]
            hb = sbuf(sa, "hb", [128, D], BF16)
            hT = [sbuf(sa, "hT%d" % i, [128, 8, 128], BF16) for i in range(2)]
            Rf = sbuf(sa, "Rf", [128, 1664], F32)
            T1 = sbuf(sa, "T1", [128, 1664], F32)
            T2 = sbuf(sa, "T2", [128, 832], F32)
            QKb = sbuf(sa, "QKb", [128, 1792], BF16)
            QTst = [sbuf(sa, "QTst%d" % i, [128, 16, 128], BF16) for i in range(2)]
            Amod = sbuf(sa, "Amod", [128, D], F32)
            Bmod = sbuf(sa, "Bmod", [128, D], F32)
            sq = sbuf(sa, "sq", [128, D], F32)
            Ct = [sbuf(sa, "Ct%d" % i, [128, 64], F32) for i in range(2)]
            St = [sbuf(sa, "St%d" % i, [128, 64], F32) for i in range(2)]
            QG = sbuf(sa, "QG", [128, 64], F32)
            KG = sbuf(sa, "KG", [128, 64], F32)
            small = sbuf(sa, "smallA", [128, 40], F32)
            r_win, r_hb, r_Rf, r_T1, r_T2, r_QKb = Reg("win"), Reg("hb"), Reg("Rf"), Reg("T1"), Reg("T2"), Reg("QKb")
            r_xt = [Reg("xt0"), Reg("xt1")]
            r_hT = [Reg("hT0"), Reg("hT1")]
            r_QTst = [Reg("QTst0"), Reg("QTst1")]
            r_AB, r_sq, r_QKG = Reg("AB"), Reg("sq"), Reg("QKG")
            r_rope = [Reg("rope0"), Reg("rope1")]
            r_ss, r_sd, r_rs, r_ssh, r_sdh, r_rsh = (Reg(n) for n in ("ss", "sd", "rs", "ssh", "sdh", "rsh"))
            r_QTd = Reg("QTd")
            for kd in range(8):
                for hh in range(2):
                    p.dma("gpsimd", win[:, kd, hh * 1152:(hh + 1) * 1152],
                          win_d[kd * 128:(kd + 1) * 128, hh * 1152:(hh + 1) * 1152], ("win", 0),
                          writes=[r_win] if (kd == 0 and hh == 0) else (), wadd=() if (kd == 0 and hh == 0) else [r_win])
            for n in range(NKT):
                pass
            p.op("gpsimd", lambda e: e.memset(Vsb[:, :, 0:516].rearrange("p n (h c) -> p n h c", c=129)[:, :, :, 128:129], 1.0),
                 writes=r_V)
            p.op("gpsimd", lambda e: e.memset(Vsb[:, :, 516:646].rearrange("p n (h c) -> p n h c", c=65)[:, :, :, 64:65], 1.0),
                 wadd=r_V)
            for i_ in range(2):
                p.op("gpsimd", lambda e, i_=i_: e.memset(QTst[i_][:], 0.0), writes=[r_QTst[i_]])
            bcast_row("sync", QG[:], modrows_d[9:10, 256:320], ("pa", 0), writes=[r_QKG])
            bcast_row("sync", KG[:], modrows_d[9:10, 320:384], ("pa", 1), wadd=[r_QKG])

            for n in range(int(_os.environ.get('NTILES', NKT))):
                is_ctx = n < 2
                sl = n % 2
                t0 = (n - 2) * 128
                src = ctx_d[n * 128:(n + 1) * 128, :] if is_ctx else x_d[t0:t0 + 128, :]
                if n == 0 or n == 2:
                    ra, rb = (2, 3) if is_ctx else (0, 1)
                    bcast_row("sync", Amod[:], modrows_d[ra:ra + 1, :], ("pa", 2), writes=[r_AB])
                    bcast_row("sync", Bmod[:], modrows_d[rb:rb + 1, :], ("pa", 3), wadd=[r_AB])
                p.dma("sync", xt[sl][:], src, ("xt", sl), writes=[r_xt[sl]])
                if not is_ctx:
                    p.dma("gpsimd", Ct[sl][:], cos_d[t0:t0 + 128, :], ("rope", sl), writes=[r_rope[sl]])
                    p.dma("gpsimd", St[sl][:], sin_d[t0:t0 + 128, :], ("rope", sl), wadd=[r_rope[sl]])
                p.op("scalar", lambda e, sl=sl: e.activation(sq[:], xt[sl][:], AF.Square, accum_out=small[:, 0:1]),
                     reads=[r_xt[sl]], writes=[r_sq, r_ss])
                p.op("scalar", lambda e: e.activation(small[:, 1:2], small[:, 0:1], AF.Sqrt, bias=EPS, scale=1.0 / D),
                     reads=[r_ss], writes=[r_sd])
                p.op("vector", lambda e: e.reciprocal(small[:, 2:3], small[:, 1:2]), reads=[r_sd], writes=[r_rs])
                p.op("vector", lambda e, sl=sl: e.scalar_tensor_tensor(T1[:, 0:D], xt[sl][:], small[:, 2:3], Amod[:],
                                                                      ALU.mult, ALU.mult),
                     reads=[r_xt[sl], r_rs, r_AB], writes=[r_T1])
                p.op("vector", lambda e: e.tensor_tensor(hb[:], T1[:, 0:D], Bmod[:], ALU.add),
                     reads=[r_T1, r_AB], writes=[r_hb])
                pb0 = PSB[0][:].bitcast(BF16).rearrange("p (k t) -> p k t", k=8)
                for kd in range(8):
                    p.op("tensor", lambda e, kd=kd: e.transpose(pb0[:, kd, :], hb[:, kd * 128:(kd + 1) * 128], identb[:]),
                         reads=[r_hb, r_const], writes=[PR[0]] if kd == 0 else (), wadd=[PR[0]] if kd else ())
                p.op("scalar", lambda e, sl=sl: e.copy(hT[sl][:], pb0), reads=[PR[0]], writes=[r_hT[sl]])
                blocks = [(1, 512, 512), (2, 1024, 512), (4, 2048, 256)] if is_ctx else \
                    [(0, 0, 512), (1, 512, 512), (2, 1024, 512), (3, 1536, 512), (4, 2048, 256)]
                for (b, c0, w) in blocks:
                    for kd in range(8):
                        p.op("tensor", lambda e, b=b, c0=c0, w=w, kd=kd, sl=sl: e.matmul(
                            PSB[1 + b][:, 0:w], hT[sl][:, kd, :], win[:, kd, c0:c0 + w], start=(kd == 0), stop=(kd == 7)),
                            reads=[r_hT[sl], r_win], writes=[PR[1 + b]] if kd == 0 else (), wadd=[PR[1 + b]] if kd else ())
                Pdq, Pdk, Pdv, Pgq, Pg = PSB[1], PSB[2], PSB[3], PSB[4], PSB[5]
                p.op("scalar", lambda e, n=n: e.copy(
                    Vsb[:, n, 0:516].rearrange("p (h c) -> p h c", c=129)[:, :, 0:128],
                    Pdv[:].rearrange("p (h c) -> p h c", c=128)), reads=[PR[3]], wadd=[r_V[n]])
                p.op("scalar", lambda e, n=n: e.copy(
                    Vsb[:, n, 516:646].rearrange("p (h c) -> p h c", c=65)[:, :, 0:64],
                    Pg[:, 128:256].rearrange("p (h c) -> p h c", c=64)), reads=[PR[5]], wadd=[r_V[n]])
                nh = 2 if is_ctx else 10
                if not is_ctx:
                    p.op("scalar", lambda e: e.activation(sq[:, 0:512], Pgq[:], AF.Square), reads=[PR[4]], writes=[r_sq])
                    p.op("scalar", lambda e: e.activation(sq[:, 512:640], Pg[:, 0:128], AF.Square), reads=[PR[5]],
                         wadd=[r_sq])
                    sqv = sq[:, 0:640]
                else:
                    p.op("scalar", lambda e: e.activation(sq[:, 0:128], Pg[:, 0:128], AF.Square), reads=[PR[5]],
                         writes=[r_sq])
                    sqv = sq[:, 0:128]
                p.op("vector", lambda e, sqv=sqv, nh=nh: e.tensor_reduce(
                    small[:, 4:4 + nh], sqv.rearrange("p (h c) -> p h c", c=64), AX.X, ALU.add),
                    reads=[r_sq], writes=[r_ssh])
                p.op("scalar", lambda e, nh=nh: e.activation(small[:, 14:14 + nh], small[:, 4:4 + nh], AF.Sqrt, bias=EPS,
                                                             scale=1.0 / 64), reads=[r_ssh], writes=[r_sdh])
                p.op("vector", lambda e, nh=nh: e.reciprocal(small[:, 24:24 + nh], small[:, 14:14 + nh]),
                     reads=[r_sdh], writes=[r_rsh])
                if is_ctx:
                    p.op("scalar", lambda e: e.copy(QKb[:, 512:1024], Pdk[:]), reads=[PR[2]], writes=[r_QKb])
                    p.op("vector", lambda e: e.tensor_tensor(
                        Rf[:, 1536:1664].rearrange("p (h c) -> p h c", c=64),
                        Pg[:, 0:128].rearrange("p (h c) -> p h c", c=64),
                        small[:, 24:26].unsqueeze(2).broadcast_to([128, 2, 64]), ALU.mult),
                        reads=[PR[5], r_rsh], writes=[r_Rf])
                    p.op("vector", lambda e: e.tensor_tensor(
                        QKb[:, 1536:1664].rearrange("p (h c) -> p h c", c=64),
                        Rf[:, 1536:1664].rearrange("p (h c) -> p h c", c=64),
                        KG[:].unsqueeze(1).broadcast_to([128, 2, 64]), ALU.mult),
                        reads=[r_Rf, r_QKG], wadd=[r_QKb])
                else:
                    p.op("scalar", lambda e: e.copy(Rf[:, 0:512], Pdq[:]), reads=[PR[1]], writes=[r_Rf])
                    p.op("scalar", lambda e: e.copy(Rf[:, 512:1024], Pdk[:]), reads=[PR[2]], wadd=[r_Rf])
                    p.op("vector", lambda e: e.tensor_tensor(
                        T1[:, 1024:1536].rearrange("p (h c) -> p h c", c=64),
                        Pgq[:].rearrange("p (h c) -> p h c", c=64),
                        small[:, 24:32].unsqueeze(2).broadcast_to([128, 8, 64]), ALU.mult),
                        reads=[PR[4], r_rsh], writes=[r_T1])
                    p.op("vector", lambda e: e.tensor_tensor(
                        T1[:, 1536:1664].rearrange("p (h c) -> p h c", c=64),
                        Pg[:, 0:128].rearrange("p (h c) -> p h c", c=64),
                        small[:, 24 + 8:24 + 10].unsqueeze(2).broadcast_to([128, 2, 64]), ALU.mult),
                        reads=[PR[5], r_rsh], wadd=[r_T1])
                    p.op("vector", lambda e: e.tensor_tensor(
                        Rf[:, 1024:1536].rearrange("p (h c) -> p h c", c=64),
                        T1[:, 1024:1536].rearrange("p (h c) -> p h c", c=64),
                        QG[:].unsqueeze(1).broadcast_to([128, 8, 64]), ALU.mult),
                        reads=[r_T1, r_QKG], wadd=[r_Rf])
                    p.op("vector", lambda e: e.tensor_tensor(
                        Rf[:, 1536:1664].rearrange("p (h c) -> p h c", c=64),
                        T1[:, 1536:1664].rearrange("p (h c) -> p h c", c=64),
                        KG[:].unsqueeze(1).broadcast_to([128, 2, 64]), ALU.mult),
                        reads=[r_T1, r_QKG], wadd=[r_Rf])
                    R3 = Rf[:].rearrange("p (h c) -> p h c", c=64)
                    p.op("vector", lambda e, sl=sl: e.tensor_tensor(
                        T1[:].rearrange("p (h c) -> p h c", c=64), R3,
                        Ct[sl][:].unsqueeze(1).broadcast_to([128, 26, 64]), ALU.mult),
                        reads=[r_Rf, r_rope[sl]], writes=[r_T1])
                    R4 = Rf[:].rearrange("p (h a s f) -> p h a s f", a=2, s=2, f=16)
                    T4 = T1[:].rearrange("p (h a s f) -> p h a s f", a=2, s=2, f=16)
                    Q4 = QKb[:, 0:1664].rearrange("p (h a s f) -> p h a s f", a=2, s=2, f=16)
                    T2v = T2[:].rearrange("p (h a f) -> p h a f", a=2, f=16)
                    for half in range(2):
                        S4 = St[sl][:].rearrange("p (a s f) -> p a s f", a=2, s=2, f=16)[:, :, half, :]
                        p.op("vector", lambda e, half=half, S4=S4: e.tensor_tensor(
                            T2v, R4[:, :, :, 1 - half, :], S4.unsqueeze(1).broadcast_to([128, 26, 2, 16]), ALU.mult),
                            reads=[r_Rf, r_rope[sl]], writes=[r_T2])
                        p.op("vector", lambda e, half=half: e.tensor_tensor(
                            Q4[:, 0:16, :, half, :], T4[:, 0:16, :, half, :], T2v[:, 0:16], ALU.add),
                            reads=[r_T1, r_T2], writes=[r_QKb] if half == 0 else (), wadd=[r_QKb] if half else ())
                        p.op("vector", lambda e, half=half: e.tensor_tensor(
                            Q4[:, 24:26, :, half, :], T4[:, 24:26, :, half, :], T2v[:, 24:26], ALU.add),
                            reads=[r_T1, r_T2], wadd=[r_QKb])
                        G6 = QKb[:, 1024:1536].rearrange("p (j k a s f) -> p j k a s f", j=4, k=2, a=2, s=2, f=16)
                        for kvh in range(2):
                            h0 = 16 + kvh * 4
                            p.op("vector", lambda e, half=half, kvh=kvh, h0=h0, G6=G6: e.tensor_tensor(
                                G6[:, :, kvh, :, half, :], T4[:, h0:h0 + 4, :, half, :], T2v[:, h0:h0 + 4], ALU.add),
                                reads=[r_T1, r_T2], wadd=[r_QKb])
                pk = PSB[7][:].bitcast(BF16).rearrange("p (k t) -> p k t", k=8)
                pq = PSB[6][:].bitcast(BF16).rearrange("p (k t) -> p k t", k=8)
                kch = [4, 5, 6, 7, 12]
                for j, ch in enumerate(kch):
                    p.op("tensor", lambda e, j=j, ch=ch: e.transpose(pk[:, j, :], QKb[:, ch * 128:(ch + 1) * 128], identb[:]),
                         reads=[r_QKb, r_const], writes=[PR[7]] if j == 0 else (), wadd=[PR[7]] if j else ())
                p.op("scalar", lambda e, n=n: e.copy(KT[:, :, n * 128:(n + 1) * 128], pk[:, 0:5, :]), reads=[PR[7]],
                     writes=[r_KT[n]])
                if not is_ctx:
                    qch = [0, 1, 2, 3, 8, 9, 10, 11]
                    for j, ch in enumerate(qch):
                        p.op("tensor", lambda e, j=j, ch=ch: e.transpose(pq[:, j, :], QKb[:, ch * 128:(ch + 1) * 128], identb[:]),
                             reads=[r_QKb, r_const], writes=[PR[6]] if j == 0 else (), wadd=[PR[6]] if j else ())
                    for cc_ in range(2):
                        lo = cc_ * 64
                        Qs = QTst[sl][lo:lo + 64, 0:8, :].rearrange("p (h c) t -> p h c t", c=2)[:, :, cc_, :]
                        p.op("vector", lambda e, lo=lo, Qs=Qs: e.tensor_copy(Qs, pq[lo:lo + 64, 0:4, :]), reads=[PR[6]],
                             writes=[r_QTst[sl]] if cc_ == 0 else (), wadd=[r_QTst[sl]] if cc_ else ())
                        p.op("vector", lambda e, lo=lo, cc_=cc_, sl=sl: e.tensor_copy(
                            QTst[sl][lo:lo + 64, 8 + cc_ * 4:12 + cc_ * 4, :], pq[lo:lo + 64, 4:8, :]), reads=[PR[6]],
                            wadd=[r_QTst[sl]])
                    p.dma("sync", QTd[:, :, t0:t0 + 128], QTst[sl][:], ("qtst", sl), reads=[r_QTst[sl]], wadd=[r_QTd])
            if KTdbg is not None:
                p.dma("sync", KTdbg, KT[:], ("dbg", 0), reads=r_KT)
            if Vdbg is not None:
                p.dma("sync", Vdbg, Vsb[:], ("dbg", 1), reads=r_V)
            p.barrier()
        if upto == "PA":
            p.emit()
            return nc
        with ExitStack() as sb_:
            wout = sbuf(sb_, "wout", [128, 8, D], BF16)
            QTb = [sbuf(sb_, "QTb%d" % i, [128, 8, 512], BF16) for i in range(2)]
            PT = [sbuf(sb_, "PT%d" % i, [128, 512], BF16) for i in range(4)]
            Ot = sbuf(sb_, "Ot", [128, 4, D], BF16)
            OT = sbuf(sb_, "OT", [128, 8, 128], BF16)
            o1n = sbuf(sb_, "o1n", [128, 4, 128], F32)
            od = sbuf(sb_, "od", [128, 4, 128], F32)
            junk = sbuf(sb_, "junk", [128, 4, 128], F32)
            xt2 = [sbuf(sb_, "xt2_%d" % i, [128, D], F32) for i in range(2)]
            tmpf = sbuf(sb_, "tmpf", [128, D], F32)
            x1t = [sbuf(sb_, "x1t%d" % i, [128, D], F32) for i in range(2)]
            h2b = sbuf(sb_, "h2b", [128, D], BF16)
            h2Tst = [sbuf(sb_, "h2Tst%d" % i, [128, 8, 128], BF16) for i in range(2)]
            G1 = sbuf(sb_, "G1", [128, D], F32)
            A2 = sbuf(sb_, "A2", [128, D], F32)
            B2 = sbuf(sb_, "B2", [128, D], F32)
            SG = sbuf(sb_, "SG", [128, 128], F32)
            lamneg = sbuf(sb_, "lamneg", [128, 1], F32)
            smb = sbuf(sb_, "smb", [128, 64], F32)
            r_wout, r_Ot, r_OT, r_o1n, r_od, r_junk = (Reg(n) for n in ("wout", "Ot", "OT", "o1n", "od", "junk"))
            r_QTb = [Reg("QTb0"), Reg("QTb1")]
            r_PT = [Reg("PT%d" % i) for i in range(4)]
            r_xt2 = [Reg("xt2_0"), Reg("xt2_1")]
            r_x1t = [Reg("x1t0"), Reg("x1t1")]
            r_h2Tst = [Reg("h2Tst0"), Reg("h2Tst1")]
            r_tmpf, r_h2b, r_cB, r_smb, r_smb2 = (Reg(n) for n in ("tmpf", "h2b", "cB", "smb", "smb2"))
            r_X1d, r_H2Td = Reg("X1d"), Reg("H2Td")
            for kc in range(8):
                p.dma("gpsimd", wout[:, kc, :], wout_d[kc * 128:(kc + 1) * 128, :], ("wout", 0),
                      writes=[r_wout] if kc == 0 else (), wadd=[r_wout] if kc else ())
            bcast_row("sync", G1[:], modrows_d[4:5, :], ("pb", 0), writes=[r_cB])
            bcast_row("sync", A2[:], modrows_d[5:6, :], ("pb", 1), wadd=[r_cB])
            bcast_row("sync", B2[:], modrows_d[6:7, :], ("pb", 2), wadd=[r_cB])
            bcast_row("sync", SG[:], modrows_d[9:10, 128:256], ("pb", 3), wadd=[r_cB])
            bcast_row("sync", lamneg[:], modrows_d[9:10, 0:1], ("pb", 4), wadd=[r_cB])

            heads = []
            for h in range(4):
                for c in range(2):
                    heads.append(dict(kind="d", h=h, c=c, kch=h, qch=h, base=c * 64, v0=h * 129, vw=129))
            for g in range(8):
                kv = g // 4
                b = (g % 2) * 64
                heads.append(dict(kind="g", g=g, kch=4, qch=4 + g // 2, base=b,
                                  v0=516 + kv * 65, vw=65))
            pairs = [(3, 4), (5, 6)]
            NQB = int(_os.environ.get("NQB", 8))
            for qb in range(NQB):
                slq = 0

                def load_q(part, qbn):
                    p.dma("sync", QTb[part][:], QTd[:, part * 8:(part + 1) * 8, qbn * 512:(qbn + 1) * 512], ("qtb", part),
                          reads=[r_QTd], writes=[r_QTb[part]])

                if qb == 0:
                    load_q(0, 0)
                    load_q(1, 0)
                steps = [(hi, c) for hi in range(len(heads)) for c in range(NKT)]

                def emit_qk(k, slq=slq):
                    hi, c = steps[k]
                    hd = heads[hi]
                    kch = hd["kch"]
                    part, m = (0, hi) if hi < 8 else (1, hi - 8)
                    sbk, pt = k % 3, k % 4
                    p.op("tensor", lambda e: e.matmul(
                        PSB[sbk][:], KT[:, kch, c * 128:(c + 1) * 128], QTb[part][:, m, :], start=True, stop=True),
                        reads=[r_KT[c], r_QTb[part]], writes=[PR[sbk]])
                    if qb + 1 < NQB and c == NKT - 1 and hi in (7, 15):
                        load_q(part, qb + 1)
                    p.op("scalar", lambda e: e.activation(PT[pt][:], PSB[sbk][:], AF.Exp, scale=0.125),
                         reads=[PR[sbk]], writes=[r_PT[pt]])

                def emit_pv(k):
                    hi, c = steps[k]
                    hd = heads[hi]
                    pair = pairs[hi % 2]
                    v0, vw = hd["v0"], hd["vw"]
                    pt = k % 4
                    for qs in range(4):
                        bank = pair[qs // 2]
                        c0 = (qs % 2) * vw
                        first = (c == 0 and qs % 2 == 0)
                        p.op("tensor", lambda e, bank=bank, c0=c0, qs=qs, first=first: e.matmul(
                            PSB[bank][:, c0:c0 + vw], PT[pt][:, qs * 128:(qs + 1) * 128], Vsb[:, c, v0:v0 + vw],
                            start=first, stop=(c == NKT - 1), skip_group_check=True),
                            reads=[r_PT[pt], r_V[c]], writes=[PR[bank]] if first else (), wadd=() if first else [PR[bank]])

                LA = 2
                for k in range(LA):
                    emit_qk(k)
                for k in range(len(steps)):
                    if k + LA < len(steps):
                        emit_qk(k + LA)
                    emit_pv(k)
                    hi, c = steps[k]
                    if c != NKT - 1:
                        continue
                    hd = heads[hi]
                    pair = pairs[hi % 2]
                    vw = hd["vw"]
                    vA = PSB[pair[0]][:, 0:2 * vw].rearrange("p (q c) -> p q c", c=vw)
                    vB = PSB[pair[1]][:, 0:2 * vw].rearrange("p (q c) -> p q c", c=vw)
                    vv = [vA, vA, vB, vB]
                    dv_ = vw - 1
                    rpair = [PR[pair[0]], PR[pair[1]]]
                    if hd["kind"] == "d" and hd["c"] == 0:
                        p.op("vector", lambda e, vA=vA, dv_=dv_: e.reciprocal(smb[:, 0:2], vA[:, :, dv_]), reads=rpair, writes=[r_smb])
                        p.op("vector", lambda e, vB=vB, dv_=dv_: e.reciprocal(smb[:, 2:4], vB[:, :, dv_]), reads=rpair, wadd=[r_smb])
                        for qs in range(4):
                            p.op("vector", lambda e, qs=qs, vv=vv: e.tensor_scalar(
                                o1n[:, qs, :], vv[qs][:, qs % 2, 0:128], smb[:, qs:qs + 1], None, ALU.mult),
                                reads=rpair + [r_smb], writes=[r_o1n] if qs == 0 else (), wadd=[r_o1n] if qs else ())
                    elif hd["kind"] == "d":
                        h = hd["h"]
                        p.op("vector", lambda e, vA=vA, dv_=dv_: e.reciprocal(smb[:, 4:6], vA[:, :, dv_]), reads=rpair, writes=[r_smb])
                        p.op("vector", lambda e, vB=vB, dv_=dv_: e.reciprocal(smb[:, 6:8], vB[:, :, dv_]), reads=rpair, wadd=[r_smb])
                        p.op("vector", lambda e: e.tensor_scalar(smb[:, 8:12], smb[:, 4:8], lamneg[:, 0:1], None, ALU.mult),
                             reads=[r_smb, r_cB], wadd=[r_smb])
                        for qs in range(4):
                            p.op("vector", lambda e, qs=qs, vv=vv: e.scalar_tensor_tensor(
                                od[:, qs, :], vv[qs][:, qs % 2, 0:128], smb[:, 8 + qs:9 + qs], o1n[:, qs, :], ALU.mult, ALU.add),
                                reads=rpair + [r_smb, r_o1n], writes=[r_od] if qs == 0 else (), wadd=[r_od] if qs else ())
                        p.op("vector", lambda e: e.tensor_tensor(junk[:], od[:], od[:], ALU.mult), reads=[r_od], writes=[r_junk])
                        p.op("vector", lambda e: e.tensor_reduce(smb[:, 12:16], junk[:], AX.X, ALU.add), reads=[r_junk], wadd=[r_smb])
                        p.op("scalar", lambda e: e.activation(smb[:, 16:20], smb[:, 12:16], AF.Sqrt, bias=EPS, scale=1.0 / 128),
                             reads=[r_smb], wadd=[r_smb])
                        p.op("vector", lambda e: e.reciprocal(smb[:, 20:24], smb[:, 16:20]), reads=[r_smb], wadd=[r_smb])
                        for qs in range(4):
                            p.op("vector", lambda e, qs=qs, h=h: e.scalar_tensor_tensor(
                                Ot[:, qs, h * 128:(h + 1) * 128], od[:, qs, :], smb[:, 20 + qs:21 + qs], SG[:], ALU.mult, ALU.mult),
                                reads=[r_od, r_smb, r_cB], wadd=[r_Ot])
                    else:
                        g = hd["g"]
                        p.op("vector", lambda e, vA=vA, dv_=dv_: e.reciprocal(smb[:, 24:26], vA[:, :, dv_]), reads=rpair, writes=[r_smb])
                        p.op("vector", lambda e, vB=vB, dv_=dv_: e.reciprocal(smb[:, 26:28], vB[:, :, dv_]), reads=rpair, wadd=[r_smb])
                        for qs in range(4):
                            p.op("vector", lambda e, qs=qs, vv=vv, g=g: e.tensor_scalar(
                                Ot[:, qs, 512 + g * 64:512 + (g + 1) * 64], vv[qs][:, qs % 2, 0:64], smb[:, 24 + qs:25 + qs], None, ALU.mult),
                                reads=rpair + [r_smb], wadd=[r_Ot])
                for qs in range(4):
                    tt = qb * 4 + qs
                    t0 = tt * 128
                    sl = tt % 2
                    p.dma("sync", xt2[sl][:], x_d[t0:t0 + 128, :], ("xt2", sl), writes=[r_xt2[sl]])
                    pO = PSB[7][:].bitcast(BF16).rearrange("p (k t) -> p k t", k=8)
                    for kc in range(8):
                        p.op("tensor", lambda e, kc=kc, qs=qs: e.transpose(pO[:, kc, :], Ot[:, qs, kc * 128:(kc + 1) * 128], identb[:]),
                             reads=[r_Ot, r_const], writes=[PR[7]] if kc == 0 else (), wadd=[PR[7]] if kc else ())
                    p.op("scalar", lambda e: e.copy(OT[:], pO), reads=[PR[7]], writes=[r_OT])
                    for half in range(2):
                        for kc in range(8):
                            p.op("tensor", lambda e, half=half, kc=kc: e.matmul(
                                PSB[half][:], OT[:, kc, :], wout[:, kc, half * 512:(half + 1) * 512], start=(kc == 0), stop=(kc == 7)),
                                reads=[r_OT, r_wout], writes=[PR[half]] if kc == 0 else (), wadd=[PR[half]] if kc else ())
                    for half in range(2):
                        p.op("vector", lambda e, half=half: e.tensor_tensor(
                            tmpf[:, half * 512:(half + 1) * 512], PSB[half][:], G1[:, half * 512:(half + 1) * 512], ALU.mult),
                            reads=[PR[half], r_cB], writes=[r_tmpf] if half == 0 else (), wadd=[r_tmpf] if half else ())
                    p.op("vector", lambda e, sl=sl: e.tensor_tensor(x1t[sl][:], tmpf[:], xt2[sl][:], ALU.add),
                         reads=[r_tmpf, r_xt2[sl]], writes=[r_x1t[sl]])
                    p.dma("sync", X1d[t0:t0 + 128, :], x1t[sl][:], ("x1st", sl), reads=[r_x1t[sl]], wadd=[r_X1d])
                    p.op("scalar", lambda e, sl=sl: e.activation(tmpf[:], x1t[sl][:], AF.Square, accum_out=smb[:, 32:33]),
                         reads=[r_x1t[sl]], writes=[r_tmpf, r_smb2])
                    p.op("scalar", lambda e: e.activation(smb[:, 33:34], smb[:, 32:33], AF.Sqrt, bias=EPS, scale=1.0 / D),
                         reads=[r_smb2], wadd=[r_smb2])
                    p.op("vector", lambda e: e.reciprocal(smb[:, 34:35], smb[:, 33:34]), reads=[r_smb2], wadd=[r_smb2])
                    p.op("vector", lambda e, sl=sl: e.scalar_tensor_tensor(tmpf[:], x1t[sl][:], smb[:, 34:35], A2[:], ALU.mult, ALU.mult),
                         reads=[r_x1t[sl], r_smb2, r_cB], writes=[r_tmpf])
                    p.op("vector", lambda e: e.tensor_tensor(h2b[:], tmpf[:], B2[:], ALU.add), reads=[r_tmpf, r_cB], writes=[r_h2b])
                    pH = PSB[2][:].bitcast(BF16).rearrange("p (k t) -> p k t", k=8)
                    for kd in range(8):
                        p.op("tensor", lambda e, kd=kd: e.transpose(pH[:, kd, :], h2b[:, kd * 128:(kd + 1) * 128], identb[:]),
                             reads=[r_h2b, r_const], writes=[PR[2]] if kd == 0 else (), wadd=[PR[2]] if kd else ())
                    p.op("scalar", lambda e, sl=sl: e.copy(h2Tst[sl][:], pH), reads=[PR[2]], writes=[r_h2Tst[sl]])
                    p.dma("sync", H2Td[:, :, t0:t0 + 128], h2Tst[sl][:], ("h2st", sl), reads=[r_h2Tst[sl]], wadd=[r_H2Td])
            p.barrier()
        sAB.close()
        if upto == "PB":
            p.emit()
            return nc

        r_UTs, r_Vs = Reg("UTs"), Reg("Vs")

        def make_pc0(stk):
            Ub = [sbuf(stk, "Ub%d" % i, [128, D], BF16) for i in range(2)]
            Vb = [sbuf(stk, "Vb%d" % i, [128, D], BF16) for i in range(2)]
            UTst = [sbuf(stk, "UTst%d" % i, [128, 8, 128], BF16) for i in range(2)]
            r_Ub = [Reg("Ub0"), Reg("Ub1")]
            r_Vb = [Reg("Vb0"), Reg("Vb1")]
            r_UTst = [Reg("UTst0"), Reg("UTst1")]

            def chunk(i):
                sl = i % 2
                p.dma("gpsimd", Ub[sl][:], pu_d[i * 128:(i + 1) * 128, :], ("ub", sl), writes=[r_Ub[sl]])
                p.dma("gpsimd", Vb[sl][:], pv_d[i * 128:(i + 1) * 128, :], ("vb", sl), writes=[r_Vb[sl]])
                pU = PSB[7][:].bitcast(BF16).rearrange("p (k t) -> p k t", k=8)
                for kd in range(8):
                    p.op("tensor", lambda e, kd=kd, sl=sl, pU=pU: e.transpose(pU[:, kd, :], Ub[sl][:, kd * 128:(kd + 1) * 128], identb[:]),
                         reads=[r_Ub[sl], r_const], writes=[PR[7]] if kd == 0 else (), wadd=[PR[7]] if kd else ())
                p.op("scalar", lambda e, sl=sl, pU=pU: e.copy(UTst[sl][:], pU), reads=[PR[7]], writes=[r_UTst[sl]])
                p.dma("sync", UTs[i], UTst[sl][:].rearrange("p k j -> p (k j)"), ("utst", sl), reads=[r_UTst[sl]], wadd=[r_UTs])
                p.dma("sync", Vs[i], Vb[sl][:], ("vst", sl), reads=[r_Vb[sl]], wadd=[r_Vs])
            return chunk

        r_IJGd = Reg("IJGd")
        with ExitStack() as sc1:
            wq = sbuf(sc1, "wq", [128, 8, 2048], BF16)
            SKT = sbuf(sc1, "SKT", [128, 16, 128], BF16)
            skf = [sbuf(sc1, "skf%d" % i, [128, 128], F32) for i in range(2)]
            h2Tb = [sbuf(sc1, "h2Tb%d" % i, [128, 8, 512], BF16) for i in range(2)]
            qT = sbuf(sc1, "qT", [128, 16, 512], BF16)
            Sf = sbuf(sc1, "Sf", [128, 16, 128], F32)
            S2 = sbuf(sc1, "S2", [128, 16, 128], F32)
            mx1 = sbuf(sc1, "mx1", [128, 16, 16], F32)
            ix1 = sbuf(sc1, "ix1", [128, 16, 16], U32)
            ixf = sbuf(sc1, "ixf", [128, 16, 16], F32)
            cand = sbuf(sc1, "cand", [128, 8, 256], F32)
            cand2 = sbuf(sc1, "cand2", [128, 8, 256], F32)
            best = sbuf(sc1, "best", [128, 8, 16], F32)
            pos = sbuf(sc1, "pos", [128, 8, 16], U32)
            posf = sbuf(sc1, "posf", [128, 8, 16], F32)
            thr16 = sbuf(sc1, "thr16", [128, 16], F32)
            k0f = sbuf(sc1, "k0f", [128, 8, 16], F32)
            k1f = sbuf(sc1, "k1f", [128, 8, 16], F32)
            eq = sbuf(sc1, "eq", [128, 8, 16, 16], F32)
            prod = sbuf(sc1, "prod", [128, 8, 16, 16], F32)
            IJGf = sbuf(sc1, "IJGf", [128, 3, 128], F32)
            e_t = sbuf(sc1, "e_t", [128, 8, 16], F32)
            sm1 = sbuf(sc1, "sm1", [128, 16], F32)
            iota16 = sbuf(sc1, "iota16", [128, 16], F32)
            IJGst = [sbuf(sc1, "IJGst%d" % i, [128, 3, 128], F32) for i in range(2)]
            r_wq, r_SKT, r_qT, r_c1 = Reg("wq"), Reg("SKT"), Reg("qT"), Reg("c1")
            r_skf = [Reg("skf0"), Reg("skf1")]
            r_h2Tb = [Reg("h2Tb0"), Reg("h2Tb1")]
            r_Sf = [Reg("Sf%d" % i) for i in range(4)]
            r_S2 = [Reg("S2_%d" % i) for i in range(16)]
            r_mx1 = [Reg("mx1_%d" % i) for i in range(16)]
            r_ix1 = [Reg("ix1_%d" % i) for i in range(16)]
            r_ixf, r_cand, r_k, r_eq, r_prod, r_IJGf, r_et, r_sm1 = (Reg(n) for n in ("ixf", "cand", "k", "eq", "prod", "IJGf", "et", "sm1"))
            r_cand2 = [Reg("cand2_%d" % i) for i in range(8)]
            r_best = [Reg("best%d" % i) for i in range(8)]
            r_pos = [Reg("pos%d" % i) for i in range(8)]
            r_IJGst = [Reg("IJGst0"), Reg("IJGst1")]
            for kd in range(8):
                for hh in range(2):
                    first = (kd == 0 and hh == 0)
                    p.dma("gpsimd", wq[:, kd, hh * 1024:(hh + 1) * 1024], wq_d[kd * 128:(kd + 1) * 128, hh * 1024:(hh + 1) * 1024],
                          ("wq", 0), writes=[r_wq] if first else (), wadd=() if first else [r_wq])
            p.dma("sync", iota16[:], iota16_d, ("c1", 0), writes=[r_c1])
            p.op("vector", lambda e: e.tensor_scalar(thr16[:], iota16[:], 16.0, 16.0, ALU.mult, ALU.add), reads=[r_c1], wadd=[r_c1])
            for ch in range(16):
                sl = ch % 2
                p.dma("sync", skf[sl][:], sk_d[ch], ("skf", sl), writes=[r_skf[sl]])
                p.op("tensor", lambda e, sl=sl: e.transpose(PSB[6 + sl][:, 0:128], skf[sl][:], identf[:]),
                     reads=[r_skf[sl], r_const], writes=[PR[6 + sl]])
                p.op("scalar", lambda e, sl=sl, ch=ch: e.copy(SKT[:, ch, :], PSB[6 + sl][:, 0:128]), reads=[PR[6 + sl]],
                     writes=[r_SKT] if ch == 0 else (), wadd=[r_SKT] if ch else ())
            pc0_chunk = make_pc0(sc1)
            NTB = int(_os.environ.get("NTB", 8))
            for tb in range(NTB):
                slb = tb % 2
                p.dma("sync", h2Tb[slb][:], H2Td[:, :, tb * 512:(tb + 1) * 512], ("h2tb", slb), reads=[r_H2Td] if "PB" in phases else (),
                      writes=[r_h2Tb[slb]])
                for ch in range(16):
                    bk = ch % 2
                    for kd in range(8):
                        p.op("tensor", lambda e, bk=bk, kd=kd, ch=ch, slb=slb: e.matmul(
                            PSB[bk][:], wq[:, kd, ch * 128:(ch + 1) * 128], h2Tb[slb][:, kd, :], start=(kd == 0), stop=(kd == 7)),
                            reads=[r_wq, r_h2Tb[slb]], writes=[PR[bk]] if kd == 0 else (), wadd=[PR[bk]] if kd else ())
                    p.op("scalar", lambda e, bk=bk, ch=ch: e.copy(qT[:, ch, :], PSB[bk][:]), reads=[PR[bk]],
                         writes=[r_qT] if ch == 0 else (), wadd=[r_qT] if ch else ())
                for ts in range(4):
                    tt = tb * 4 + ts
                    t0 = tt * 128
                    sl = tt % 2
                    for k4 in range(4):
                        pc0_chunk(tt * 4 + k4)
                    for ch in range(16):
                        bk = 2 + ch // 4
                        p.op("tensor", lambda e, bk=bk, ch=ch, ts=ts: e.matmul(
                            PSB[bk][:, (ch % 4) * 128:(ch % 4 + 1) * 128], qT[:, ch, ts * 128:(ts + 1) * 128], SKT[:, ch, :],
                            start=True, stop=True, skip_group_check=True),
                            reads=[r_qT, r_SKT], writes=[PR[bk]] if ch % 4 == 0 else (), wadd=[PR[bk]] if ch % 4 else ())
                    for b4 in range(4):
                        p.op("scalar", lambda e, b4=b4: e.copy(Sf[:, b4 * 4:(b4 + 1) * 4, :], PSB[2 + b4][:].rearrange("p (c n) -> p c n", c=4)),
                             reads=[PR[2 + b4]], writes=[r_Sf[b4]])
                    for g in range(16):
                        p.op("vector", lambda e, g=g: e.max(mx1[:, g, 0:8], Sf[:, g, :]), reads=[r_Sf[g // 4]], writes=[r_mx1[g]], soft=True)
                    for g in range(16):
                        p.op("vector", lambda e, g=g: e.max_index(ix1[:, g, 0:8], mx1[:, g, 0:8], Sf[:, g, :]),
                             reads=[r_Sf[g // 4], r_mx1[g]], writes=[r_ix1[g]], soft=True)
                    for g in range(16):
                        p.op("vector", lambda e, g=g: e.match_replace(S2[:, g, :], mx1[:, g, 0:8], Sf[:, g, :], -1e30),
                             reads=[r_Sf[g // 4], r_mx1[g]], writes=[r_S2[g]], soft=True)
                    for g in range(16):
                        p.op("vector", lambda e, g=g: e.max(mx1[:, g, 8:16], S2[:, g, :]), reads=[r_S2[g]], wadd=[r_mx1[g]], soft=True)
                    for g in range(16):
                        p.op("vector", lambda e, g=g: e.max_index(ix1[:, g, 8:16], mx1[:, g, 8:16], S2[:, g, :]),
                             reads=[r_S2[g], r_mx1[g]], wadd=[r_ix1[g]], soft=True)
                    p.op("vector", lambda e: e.tensor_copy(ixf[:], ix1[:]), reads=r_ix1, writes=[r_ixf])
                    mxv = mx1[:].rearrange("p (h two) k -> p h two k", two=2)
                    p.op("vector", lambda e, mxv=mxv: e.tensor_tensor(
                        cand[:].rearrange("p h (a b) -> p h a b", a=16),
                        mxv[:, :, 0, :].unsqueeze(3).broadcast_to([128, 8, 16, 16]),
                        mxv[:, :, 1, :].unsqueeze(2).broadcast_to([128, 8, 16, 16]), ALU.add),
                        reads=r_mx1, writes=[r_cand])
                    for h in range(8):
                        p.op("vector", lambda e, h=h: e.max(best[:, h, 0:8], cand[:, h, :]), reads=[r_cand], writes=[r_best[h]], soft=True)
                    for h in range(8):
                        p.op("vector", lambda e, h=h: e.max_index(pos[:, h, 0:8], best[:, h, 0:8], cand[:, h, :]),
                             reads=[r_cand, r_best[h]], writes=[r_pos[h]], soft=True)
                    for h in range(8):
                        p.op("vector", lambda e, h=h: e.match_replace(cand2[:, h, :], best[:, h, 0:8], cand[:, h, :], -1e30),
                             reads=[r_cand, r_best[h]], writes=[r_cand2[h]], soft=True)
                    for h in range(8):
                        p.op("vector", lambda e, h=h: e.max(best[:, h, 8:16], cand2[:, h, :]), reads=[r_cand2[h]], wadd=[r_best[h]], soft=True)
                    for h in range(8):
                        p.op("vector", lambda e, h=h: e.max_index(pos[:, h, 8:16], best[:, h, 8:16], cand2[:, h, :]),
                             reads=[r_cand2[h], r_best[h]], wadd=[r_pos[h]], soft=True)
                    p.op("vector", lambda e: e.tensor_copy(posf[:], pos[:]), reads=r_pos, writes=[r_k])
                    p.op("vector", lambda e: e.tensor_tensor(
                        eq[:], posf[:].unsqueeze(3).broadcast_to([128, 8, 16, 16]),
                        thr16[:].unsqueeze(1).unsqueeze(1).broadcast_to([128, 8, 16, 16]), ALU.is_ge),
                        reads=[r_k, r_c1], writes=[r_eq])
                    p.op("vector", lambda e: e.tensor_reduce(k0f[:], eq[:], AX.X, ALU.add), reads=[r_eq], wadd=[r_k])
                    p.op("vector", lambda e: e.scalar_tensor_tensor(k1f[:], k0f[:], -16.0, posf[:], ALU.mult, ALU.add),
                         reads=[r_k], wadd=[r_k])
                    ixv = ixf[:].rearrange("p (h two) k -> p h two k", two=2)
                    io4 = iota16[:].unsqueeze(1).unsqueeze(1).broadcast_to([128, 8, 16, 16])
                    for which, kf in ((0, k0f), (1, k1f)):
                        p.op("vector", lambda e, kf=kf: e.tensor_tensor(
                            eq[:], kf[:].unsqueeze(3).broadcast_to([128, 8, 16, 16]), io4, ALU.is_equal),
                            reads=[r_k, r_c1], writes=[r_eq])
                        p.op("vector", lambda e, which=which, ixv=ixv: e.tensor_tensor(
                            prod[:], eq[:], ixv[:, :, which, :].unsqueeze(2).broadcast_to([128, 8, 16, 16]), ALU.mult),
                            reads=[r_eq, r_ixf], writes=[r_prod])
                        p.op("vector", lambda e, which=which: e.tensor_reduce(
                            IJGf[:, which, :].rearrange("p (h s) -> p h s", h=8), prod[:], AX.X, ALU.add),
                            reads=[r_prod], writes=[r_IJGf] if which == 0 else (), wadd=[r_IJGf] if which else ())
                    p.op("vector", lambda e: e.tensor_tensor(e_t[:], best[:], best[:, :, 0:1].broadcast_to([128, 8, 16]), ALU.subtract),
                         reads=r_best, writes=[r_et])
                    p.op("scalar", lambda e: e.activation(e_t[:], e_t[:], AF.Exp), reads=[r_et], writes=[r_et])
                    p.op("vector", lambda e: e.tensor_reduce(sm1[:, 0:8], e_t[:], AX.X, ALU.add), reads=[r_et], writes=[r_sm1])
                    p.op("vector", lambda e: e.reciprocal(sm1[:, 8:16], sm1[:, 0:8]), reads=[r_sm1], wadd=[r_sm1])
                    p.op("vector", lambda e: e.tensor_tensor(
                        IJGf[:, 2, :].rearrange("p (h s) -> p h s", h=8), e_t[:],
                        sm1[:, 8:16].unsqueeze(2).broadcast_to([128, 8, 16]), ALU.mult),
                        reads=[r_et, r_sm1], wadd=[r_IJGf])
                    for k in range(3):
                        p.op("tensor", lambda e, k=k: e.transpose(PSB[6][:, k * 128:(k + 1) * 128], IJGf[:, k, :], identf[:]),
                             reads=[r_IJGf, r_const], writes=[PR[6]] if k == 0 else (), wadd=[PR[6]] if k else ())
                    p.op("scalar", lambda e, sl=sl: e.copy(IJGst[sl][:], PSB[6][:, 0:384].rearrange("p (k t) -> p k t", k=3)),
                         reads=[PR[6]], writes=[r_IJGst[sl]])
                    p.dma("sync", IJGd[:, :, t0:t0 + 128], IJGst[sl][:], ("ijgst", sl), reads=[r_IJGst[sl]], wadd=[r_IJGd])
            if UVdbg is not None:
                dbt = sbuf(sc1, "dbt", [128, D], BF16)
                r_dbt = Reg("dbt")
                for k, (src, idx) in enumerate(((UTs, 0), (UTs, 77), (UTs, 127), (Vs, 5))):
                    p.dma("sync", dbt[:], src[idx], ("dbg", 2), reads=[r_UTs, r_Vs], writes=[r_dbt])
                    p.dma("sync", UVdbg[k], dbt[:], ("dbg", 3), reads=[r_dbt])
            p.barrier()
        if upto == "PC1":
            p.emit()
            return nc

        with ExitStack() as sc2:
            GT = [sbuf(sc2, "GT%d" % i, [128, 128, 256], BF16) for i in range(2)]
            GSZ = 2
            Ust = [sbuf(sc2, "Ust%d" % i, [128, GSZ, D], BF16) for i in range(2)]
            Vst = [sbuf(sc2, "Vst%d" % i, [128, GSZ, D], BF16) for i in range(2)]
            IJGt = [sbuf(sc2, "IJGt%d" % i, [128, 3, 256], F32) for i in range(2)]
            h2Tt = [sbuf(sc2, "h2Tt%d" % i, [128, 8, 256], BF16) for i in range(2)]
            x1t2 = sbuf(sc2, "x1t2", [128, 2, D], F32)
            OJ = [sbuf(sc2, "OJ%d" % i, [128, 128], BF16) for i in range(4)]
            OIg = [sbuf(sc2, "OIg%d" % i, [128, 128], BF16) for i in range(4)]
            gA = [sbuf(sc2, "gA%d" % i, [128, 256], BF16) for i in range(2)]
            Wt = [sbuf(sc2, "Wt%d" % i, [128, 256], BF16) for i in range(2)]
            G2t = sbuf(sc2, "G2t", [128, D], F32)
            Gft = sbuf(sc2, "Gft", [128, D], F32)
            tmp2 = sbuf(sc2, "tmp2", [128, D], F32)
            outt = sbuf(sc2, "outt", [128, D], F32)
            iotab = sbuf(sc2, "iotab", [128, 128], BF16)
            smc = sbuf(sc2, "smc", [128, 8], F32)
            r_x1t2, r_c2, r_tmp2, r_smc, r_outt = (Reg(n) for n in ("x1t2", "c2", "tmp2", "smc", "outt"))
            r_GT = [Reg("GT0"), Reg("GT1")]
            r_Ust, r_Vst = [Reg("Ust0"), Reg("Ust1")], [Reg("Vst0"), Reg("Vst1")]
            r_IJGt, r_h2Tt = [Reg("IJGt0"), Reg("IJGt1")], [Reg("h2Tt0"), Reg("h2Tt1")]
            r_OJ = [Reg("OJ%d" % i) for i in range(4)]
            r_OIg = [Reg("OIg%d" % i) for i in range(4)]
            r_gA, r_Wt = [Reg("gA0"), Reg("gA1")], [Reg("Wt0"), Reg("Wt1")]
            r_out = Reg("out")
            p.dma("sync", iotab[:], iotab_d, ("c2", 0), writes=[r_c2])
            bcast_row("sync", G2t[:], modrows_d[7:8, :], ("c2", 1), wadd=[r_c2])
            bcast_row("sync", Gft[:], modrows_d[8:9, :], ("c2", 2), wadd=[r_c2])
            NTL = int(_os.environ.get("NTL", 16))
            NI = 128
            NG = NI // GSZ

            def load_ijg(T):
                s_ = T % 2
                p.dma("sync", IJGt[s_][:], IJGd[:, :, T * 256:(T + 1) * 256], ("ijgt", s_), reads=[r_IJGd], writes=[r_IJGt[s_]])

            def gb_dve(T, c):
                s_ = T % 2
                s4 = c % 4
                p.op("vector", lambda e: e.tensor_scalar(OJ[s4][:], iotab[:], IJGt[s_][:, 1, c:c + 1], None, ALU.is_equal),
                     reads=[r_IJGt[s_], r_c2], writes=[r_OJ[s4]])
                p.op("vector", lambda e: e.tensor_scalar(OIg[s4][:], iotab[:], IJGt[s_][:, 0, c:c + 1], IJGt[s_][:, 2, c:c + 1],
                                                         ALU.is_equal, ALU.mult),
                     reads=[r_IJGt[s_], r_c2], writes=[r_OIg[s4]])

            def gb_pe(T, c):
                s_ = T % 2
                s4 = c % 4
                bk = 6 + (c // 4) % 2
                p.op("tensor", lambda e: e.matmul(PSB[bk][:, (c % 4) * 128:(c % 4 + 1) * 128], OJ[s4][:], OIg[s4][:],
                                                 start=True, stop=True, skip_group_check=True),
                     reads=[r_OJ[s4], r_OIg[s4]], writes=[PR[bk]] if c % 4 == 0 else (), wadd=[PR[bk]] if c % 4 else ())
                if c % 4 == 3:
                    cb = c - 3
                    p.op("scalar", lambda e: e.copy(
                        GT[s_][:, :, cb:cb + 4].rearrange("p i c -> p c i"), PSB[bk][:].rearrange("p (c i) -> p c i", c=4)),
                        reads=[PR[bk]], writes=[r_GT[s_]] if c == 3 else (), wadd=[r_GT[s_]] if c != 3 else ())

            load_ijg(0)
            for c in range(256):
                gb_dve(0, c)
                gb_pe(0, c)
            for T in range(NTL):
                s = T % 2
                c00 = T * 256
                p.dma("sync", h2Tt[s][:], H2Td[:, :, c00:c00 + 256], ("h2tt", s), writes=[r_h2Tt[s]])
                if T + 1 < NTL:
                    load_ijg(T + 1)

                def stream(ig):
                    sg = ig % 2
                    p.dma("sync", Ust[sg][:], UTs[ig * GSZ:(ig + 1) * GSZ].rearrange("g p d -> p g d"), ("ust", sg),
                          reads=[r_UTs], writes=[r_Ust[sg]])
                    p.dma("sync", Vst[sg][:], Vs[ig * GSZ:(ig + 1) * GSZ].rearrange("g p d -> p g d"), ("vstr", sg),
                          reads=[r_Vs], writes=[r_Vst[sg]])

                def emit_A(i, s=s):
                    ig, ii = i // GSZ, i % GSZ
                    sg = ig % 2
                    ab = i % 2
                    for kd in range(8):
                        p.op("tensor", lambda e, kd=kd: e.matmul(
                            PSB[ab][:, 0:256], Ust[sg][:, ii, kd * 128:(kd + 1) * 128], h2Tt[s][:, kd, :], start=(kd == 0), stop=(kd == 7)),
                            reads=[r_Ust[sg], r_h2Tt[s]], writes=[PR[ab]] if kd == 0 else (), wadd=[PR[ab]] if kd else ())
                    p.op("scalar", lambda e: e.activation(gA[ab][:], PSB[ab][:, 0:256], AF.Gelu), reads=[PR[ab]], writes=[r_gA[ab]])
                    p.op("vector", lambda e: e.tensor_tensor(Wt[ab][:], gA[ab][:], GT[s][:, i, :], ALU.mult),
                         reads=[r_gA[ab], r_GT[s]], writes=[r_Wt[ab]])

                def emit_out(i):
                    ig, ii = i // GSZ, i % GSZ
                    sg = ig % 2
                    ab = i % 2
                    for sub in range(2):
                        for half in range(2):
                            bk = 2 + sub * 2 + half
                            p.op("tensor", lambda e, bk=bk, sub=sub, half=half: e.matmul(
                                PSB[bk][:], Wt[ab][:, sub * 128:(sub + 1) * 128], Vst[sg][:, ii, half * 512:(half + 1) * 512],
                                start=(i == 0), stop=(i == NI - 1)),
                                reads=[r_Wt[ab], r_Vst[sg]], writes=[PR[bk]] if i == 0 else (), wadd=[PR[bk]] if i else ())

                stream(0)
                emit_A(0)
                for i in range(NI):
                    if i % GSZ == 0 and i // GSZ + 1 < NG:
                        stream(i // GSZ + 1)
                    if T + 1 < NTL:
                        gb_dve(T + 1, 2 * i)
                        gb_dve(T + 1, 2 * i + 1)
                    if i + 1 < NI:
                        emit_A(i + 1)
                    emit_out(i)
                    if T + 1 < NTL:
                        gb_pe(T + 1, 2 * i)
                        gb_pe(T + 1, 2 * i + 1)
                p.dma("sync", x1t2[:], X1d[c00:c00 + 256, :].rearrange("(s p) d -> p s d", p=128), ("x1t2", 0), writes=[r_x1t2])
                for sub in range(2):
                    tt = T * 2 + sub
                    t0 = tt * 128
                    for half in range(2):
                        bk = 2 + sub * 2 + half
                        p.op("vector", lambda e, bk=bk, half=half: e.tensor_tensor(
                            tmp2[:, half * 512:(half + 1) * 512], PSB[bk][:], G2t[:, half * 512:(half + 1) * 512], ALU.mult),
                            reads=[PR[bk], r_c2], writes=[r_tmp2] if half == 0 else (), wadd=[r_tmp2] if half else ())
                    p.op("vector", lambda e, sub=sub: e.tensor_tensor(tmp2[:], tmp2[:], x1t2[:, sub, :], ALU.add),
                         reads=[r_tmp2, r_x1t2], writes=[r_tmp2])
                    p.op("scalar", lambda e: e.activation(outt[:], tmp2[:], AF.Square, accum_out=smc[:, 0:1]), reads=[r_tmp2],
                         writes=[r_outt, r_smc])
                    p.op("scalar", lambda e: e.activation(smc[:, 1:2], smc[:, 0:1], AF.Sqrt, bias=EPS, scale=1.0 / D), reads=[r_smc], wadd=[r_smc])
                    p.op("vector", lambda e: e.reciprocal(smc[:, 2:3], smc[:, 1:2]), reads=[r_smc], wadd=[r_smc])
                    p.op("vector", lambda e: e.scalar_tensor_tensor(outt[:], tmp2[:], smc[:, 2:3], Gft[:], ALU.mult, ALU.mult),
                         reads=[r_tmp2, r_smc, r_c2], writes=[r_outt])
                    p.dma("sync", out_d[t0:t0 + 128, :], outt[:], ("outst", 0), reads=[r_outt], wadd=[r_out])
            p.barrier()
        p.emit()
        return nc


def _host_inputs(inputs):
    f = lambda k: np.ascontiguousarray(np.asarray(inputs[k], dtype=np.float32))
    x, c, ctx, c_ctx = f("x"), f("c"), f("ctx"), f("c_ctx")
    vecs = np.zeros((1, 8 * D), np.float32)
    vecs[0, 0:D] = f("norm1_g")[0]
    vecs[0, D:2 * D] = f("norm2_g")[0]
    vecs[0, 2 * D:3 * D] = f("final_norm_g")
    vecs[0, 3 * D:3 * D + 128] = f("diff_subln_g")[0]
    vecs[0, 4 * D:4 * D + 64] = f("gqa_q_norm_g")[0]
    vecs[0, 5 * D:5 * D + 64] = f("gqa_k_norm_g")[0]
    vecs[0, 6 * D:6 * D + 256] = np.concatenate([f("diff_lq1")[0], f("diff_lk1")[0], f("diff_lq2")[0], f("diff_lk2")[0]])
    C, Sg = _rope_tables()
    shared = {
        "w_mod": f("w_mod")[0], "b_mod": f("b_mod")[0][None, :], "vecs": vecs,
        "w_in": f("w_in")[0], "w_out": f("w_out")[0], "peer_wq": f("peer_wq")[0],
        "subkeys": np.ascontiguousarray(f("peer_subkeys")[0].reshape(16, 128, 128)),
        "peer_u": f("peer_u")[0], "peer_v": f("peer_v")[0],
        "ropec": C, "ropes": Sg,
        "identb": np.eye(128, dtype=np.float32).astype(ml_dtypes.bfloat16),
        "identf": np.eye(128, dtype=np.float32),
        "iotab": np.tile(np.arange(128, dtype=np.float32)[None, :], (128, 1)).astype(ml_dtypes.bfloat16),
        "iota16": np.tile(np.arange(16, dtype=np.float32)[None, :], (128, 1)),
    }
    maps = []
    for b in range(8):
        cc = np.stack([c[b].reshape(8, 128).T, c_ctx.reshape(8, 128).T], axis=-1)
        m = dict(shared)
        m["x"] = x[b]
        m["ctx"] = ctx[b]
        m["cc"] = np.ascontiguousarray(cc.astype(np.float32))
        maps.append(m)
    return maps


def kernel(**inputs):
    maps = _host_inputs(inputs)
    nc = build()
    maps = [{k: m[k] for k in nc._used_inputs} for m in maps]
    res = run_bass_kernel_spmd(nc, maps, core_ids=list(range(8)))
    return np.stack([np.asarray(r["out"], dtype=np.float32) for r in res.results], axis=0)
```
